# Optimizing a Trainium2 kernel written in Bass

```python
import math
import jax, jax.numpy as jnp
from jax import lax
import numpy as np

D_MODEL = 1024
BATCH = 8
SEQ = 4096
DEPTH = 4

N_MIXERS = 3
HEAD_DIM = 64
N_HEADS = D_MODEL // HEAD_DIM
MIX_WIDTH = N_HEADS * HEAD_DIM
ROT_DIM = HEAD_DIM // 4
ROPE_THETA = 500000.0
Q_BLOCK = 128
IDX_HEADS = 8
IDX_DIM = 64
IDX_ROT = IDX_DIM // 4
TOPK_MAX = 256
DSA_BLOCK = 32
FOX_HEADS = N_HEADS
MLA_HEADS = N_HEADS
MLA_NOPE = 64
MLA_ROPE = 32
MLA_V = 64
Q_LORA = 384
KV_LORA = 256
D_FF = 2816
ALPHA = (2.0 * DEPTH) ** 0.25
BETA = (8.0 * DEPTH) ** -0.25
LN_EPS = 1e-5
RMS_EPS = 1e-6
N_DSA = (DEPTH + 2) // 3
N_FOX = (DEPTH + 1) // 3
N_MLA = DEPTH // 3
DSA_IN = 3 * MIX_WIDTH + IDX_HEADS * IDX_DIM + IDX_DIM + IDX_HEADS
FOX_IN = 3 * MIX_WIDTH + FOX_HEADS
MLA_IN = Q_LORA + KV_LORA + MLA_ROPE

kernel_name = "hybrid_dsa_fox_mla_macaron_deepnorm"


def layer_norm(x, g, b):
    xf = x.astype(jnp.float32)
    mu = jnp.mean(xf, axis=-1, keepdims=True)
    var = jnp.mean(jnp.square(xf - mu), axis=-1, keepdims=True)
    return ((xf - mu) * lax.rsqrt(var + LN_EPS) * g.astype(jnp.float32) + b.astype(jnp.float32)).astype(x.dtype)


def rms_norm(x, g):
    xf = x.astype(jnp.float32)
    ms = jnp.mean(jnp.square(xf), axis=-1, keepdims=True)
    return (xf * lax.rsqrt(ms + RMS_EPS) * g.astype(jnp.float32)).astype(x.dtype)


def rope_tables(seq_len, dim):
    inv = ROPE_THETA ** (-jnp.arange(0, dim, 2, dtype=jnp.float32) / dim)
    ang = jnp.arange(seq_len, dtype=jnp.float32)[:, None] * inv[None, :]
    return jnp.cos(ang), jnp.sin(ang)


def apply_rope(x, cos, sin):
    half = x.shape[-1] // 2
    x1, x2 = x[..., :half], x[..., half:]
    c = cos[None, :, None, :].astype(x.dtype)
    s = sin[None, :, None, :].astype(x.dtype)
    return jnp.concatenate([x1 * c - x2 * s, x1 * s + x2 * c], axis=-1)


def partial_rope(x, cos, sin, rot):
    return jnp.concatenate([apply_rope(x[..., :rot], cos, sin), x[..., rot:]], axis=-1)


def swiglu(x, w13, w2):
    gate, up = jnp.split(x @ w13, 2, axis=-1)
    return (jax.nn.silu(gate) * up) @ w2


def causal_block_attention(q, k, v, scale, cum=None):
    S = q.shape[1]
    outs = []
    for i in range(S // Q_BLOCK):
        lo, hi = i * Q_BLOCK, (i + 1) * Q_BLOCK
        s = jnp.einsum('bqhd,bkhd->bhqk', q[:, lo:hi], k[:, :hi]).astype(jnp.float32) * scale
        if cum is not None:
            s = s + cum[:, :, lo:hi, None] - cum[:, :, None, :hi]
        mask = jnp.arange(lo, hi)[:, None] >= jnp.arange(hi)[None, :]
        p = jax.nn.softmax(jnp.where(mask, s, -jnp.inf), axis=-1).astype(v.dtype)
        outs.append(jnp.einsum('bhqk,bkhd->bqhd', p, v[:, :hi]))
    return jnp.concatenate(outs, axis=1)


def dsa_mixer(x, w_in, w_out, cos_p, sin_p):
    B, S, _ = x.shape
    d = MIX_WIDTH
    di = IDX_HEADS * IDX_DIM
    q, k, v, qi, ki, wi = jnp.split(x @ w_in, [d, 2 * d, 3 * d, 3 * d + di, 3 * d + di + IDX_DIM], axis=-1)
    q = partial_rope(q.reshape(B, S, N_HEADS, HEAD_DIM), cos_p, sin_p, ROT_DIM)
    k = partial_rope(k.reshape(B, S, N_HEADS, HEAD_DIM), cos_p, sin_p, ROT_DIM)
    v = v.reshape(B, S, N_HEADS, HEAD_DIM)
    qi = partial_rope(qi.reshape(B, S, IDX_HEADS, IDX_DIM), cos_p, sin_p, IDX_ROT)
    ki = partial_rope(ki.reshape(B, S, 1, IDX_DIM), cos_p, sin_p, IDX_ROT)[:, :, 0]
    wi = wi.astype(jnp.float32) * (IDX_HEADS ** -0.5)
    topk = min(TOPK_MAX, S // 4)
    scale = HEAD_DIM ** -0.5
    key_pos = jnp.arange(S)

    def block(i):
        lo = i * DSA_BLOCK
        t = lo + jnp.arange(DSA_BLOCK)
        qb = lax.dynamic_slice_in_dim(q, lo, DSA_BLOCK, axis=1)
        qib = lax.dynamic_slice_in_dim(qi, lo, DSA_BLOCK, axis=1)
        wib = lax.dynamic_slice_in_dim(wi, lo, DSA_BLOCK, axis=1)
        dots = jnp.einsum('bqhd,bkd->bqhk', qib, ki).astype(jnp.float32) * (IDX_DIM ** -0.5)
        idx_score = jnp.einsum('bqh,bqhk->bqk', wib, jax.nn.relu(dots))
        idx_score = jnp.where(t[:, None] >= key_pos[None, :], idx_score, -jnp.inf)
        _, sel = lax.top_k(idx_score, topk)
        ks = jax.vmap(lambda kk, ii: kk[ii])(k, sel)
        vs = jax.vmap(lambda vv, ii: vv[ii])(v, sel)
        logits = jnp.einsum('bqhd,bqkhd->bhqk', qb, ks).astype(jnp.float32) * scale
        valid = sel <= t[None, :, None]
        p = jax.nn.softmax(jnp.where(valid[:, None], logits, -jnp.inf), axis=-1).astype(v.dtype)
        return jnp.einsum('bhqk,bqkhd->bqhd', p, vs)

    out = lax.map(block, jnp.arange(S // DSA_BLOCK))
    out = jnp.transpose(out, (1, 0, 2, 3, 4)).reshape(B, S, d)
    return out @ w_out


def fox_mixer(x, w_in, b_f, w_out):
    B, S, _ = x.shape
    d = MIX_WIDTH
    q, k, v, f = jnp.split(x @ w_in, [d, 2 * d, 3 * d], axis=-1)
    q = q.reshape(B, S, FOX_HEADS, HEAD_DIM)
    k = k.reshape(B, S, FOX_HEADS, HEAD_DIM)
    v = v.reshape(B, S, FOX_HEADS, HEAD_DIM)
    log_f = jax.nn.log_sigmoid(f.astype(jnp.float32) + b_f.astype(jnp.float32))
    cum = jnp.transpose(jnp.cumsum(log_f, axis=1), (0, 2, 1))
    o = causal_block_attention(q, k, v, HEAD_DIM ** -0.5, cum)
    return o.reshape(B, S, d) @ w_out


def mla_mixer(x, w_dqkv, q_norm_g, w_uq, kv_norm_g, w_ukv, w_out, cos_m, sin_m):
    B, S, _ = x.shape
    cq, ckv, k_rope = jnp.split(x @ w_dqkv, [Q_LORA, Q_LORA + KV_LORA], axis=-1)
    q = (rms_norm(cq, q_norm_g) @ w_uq).reshape(B, S, MLA_HEADS, MLA_NOPE + MLA_ROPE)
    q_nope, q_rope = q[..., :MLA_NOPE], apply_rope(q[..., MLA_NOPE:], cos_m, sin_m)
    kv = (rms_norm(ckv, kv_norm_g) @ w_ukv).reshape(B, S, MLA_HEADS, MLA_NOPE + MLA_V)
    k_nope, v = kv[..., :MLA_NOPE], kv[..., MLA_NOPE:]
    k_rope = apply_rope(k_rope[:, :, None, :], cos_m, sin_m)
    q = jnp.concatenate([q_nope, q_rope], axis=-1)
    k = jnp.concatenate([k_nope, jnp.broadcast_to(k_rope, (B, S, MLA_HEADS, MLA_ROPE))], axis=-1)
    o = causal_block_attention(q, k, v, (MLA_NOPE + MLA_ROPE) ** -0.5)
    return o.reshape(B, S, MLA_HEADS * MLA_V) @ w_out


def setup_inputs(seed: int = 0) -> dict:
    key = jax.random.key(seed)
    ks = jax.random.split(key, 20)
    nrm = lambda k, shape, fan_in, s=1.0: jax.random.normal(k, shape, jnp.float32) * (fan_in ** -0.5) * s
    x = jax.random.normal(ks[0], (BATCH, SEQ, D_MODEL), jnp.float32)
    ffn1_w13 = nrm(ks[1], (DEPTH, D_MODEL, 2 * D_FF), D_MODEL)
    ffn1_w2 = nrm(ks[2], (DEPTH, D_FF, D_MODEL), D_FF, BETA)
    ffn2_w13 = nrm(ks[3], (DEPTH, D_MODEL, 2 * D_FF), D_MODEL)
    ffn2_w2 = nrm(ks[4], (DEPTH, D_FF, D_MODEL), D_FF, BETA)
    ln_g = 1.0 + 0.02 * jax.random.normal(ks[5], (DEPTH, 3, D_MODEL), jnp.float32)
    ln_b = 0.02 * jax.random.normal(ks[6], (DEPTH, 3, D_MODEL), jnp.float32)
    w_out = nrm(ks[7], (DEPTH, MIX_WIDTH, D_MODEL), MIX_WIDTH, BETA)
    dsa_w_in = nrm(ks[8], (N_DSA, D_MODEL, DSA_IN), D_MODEL)
    fox_w_in = nrm(ks[9], (N_FOX, D_MODEL, FOX_IN), D_MODEL)
    fox_b_f = (jnp.broadcast_to(jnp.linspace(1.0, 6.0, FOX_HEADS, dtype=jnp.float32), (N_FOX, FOX_HEADS))
               + 0.1 * jax.random.normal(ks[10], (N_FOX, FOX_HEADS), jnp.float32))
    mla_w_dqkv = nrm(ks[11], (N_MLA, D_MODEL, MLA_IN), D_MODEL)
    mla_q_norm_g = 1.0 + 0.02 * jax.random.normal(ks[12], (N_MLA, Q_LORA), jnp.float32)
    mla_w_uq = nrm(ks[13], (N_MLA, Q_LORA, MLA_HEADS * (MLA_NOPE + MLA_ROPE)), Q_LORA)
    mla_kv_norm_g = 1.0 + 0.02 * jax.random.normal(ks[14], (N_MLA, KV_LORA), jnp.float32)
    mla_w_ukv = nrm(ks[15], (N_MLA, KV_LORA, MLA_HEADS * (MLA_NOPE + MLA_V)), KV_LORA)
    return {"x": x, "ffn1_w13": ffn1_w13, "ffn1_w2": ffn1_w2, "ffn2_w13": ffn2_w13, "ffn2_w2": ffn2_w2,
            "ln_g": ln_g, "ln_b": ln_b, "w_out": w_out, "dsa_w_in": dsa_w_in, "fox_w_in": fox_w_in,
            "fox_b_f": fox_b_f, "mla_w_dqkv": mla_w_dqkv, "mla_q_norm_g": mla_q_norm_g,
            "mla_w_uq": mla_w_uq, "mla_kv_norm_g": mla_kv_norm_g, "mla_w_ukv": mla_w_ukv}


def reference(x, ffn1_w13, ffn1_w2, ffn2_w13, ffn2_w2, ln_g, ln_b, w_out, dsa_w_in, fox_w_in,
              fox_b_f, mla_w_dqkv, mla_q_norm_g, mla_w_uq, mla_kv_norm_g, mla_w_ukv):
    S = x.shape[1]
    cos_p, sin_p = rope_tables(S, ROT_DIM)
    cos_m, sin_m = rope_tables(S, MLA_ROPE)
    for i in range(DEPTH):
        x = layer_norm(ALPHA * x + 0.5 * swiglu(x, ffn1_w13[i], ffn1_w2[i]), ln_g[i, 0], ln_b[i, 0])
        kind, j = i % N_MIXERS, i // N_MIXERS
        if kind == 0:
            y = dsa_mixer(x, dsa_w_in[j], w_out[i], cos_p, sin_p)
        elif kind == 1:
            y = fox_mixer(x, fox_w_in[j], fox_b_f[j], w_out[i])
        else:
            y = mla_mixer(x, mla_w_dqkv[j], mla_q_norm_g[j], mla_w_uq[j], mla_kv_norm_g[j],
                          mla_w_ukv[j], w_out[i], cos_m, sin_m)
        x = layer_norm(ALPHA * x + y, ln_g[i, 1], ln_b[i, 1])
        x = layer_norm(ALPHA * x + 0.5 * swiglu(x, ffn2_w13[i], ffn2_w2[i]), ln_g[i, 2], ln_b[i, 2])
    return x
```

```python
import numpy as np
import concourse.bass as bass
import concourse.mybir as mybir
from concourse.bass_utils import run_bass_kernel_spmd

F32 = mybir.dt.float32
BF16 = mybir.dt.bfloat16
AF = mybir.ActivationFunctionType
ALU = mybir.AluOpType
AX = mybir.AxisListType

D = 1024
KC = 8
T = 512
DFF = 2816
FC = 22
H = 16
DEPTH = 4
ALPHA = (2.0 * DEPTH) ** 0.25
LN_EPS = 1e-5
RMS_EPS = 1e-6
TOPK = 256
SLOT = 4096
NSLOT = 4
ROPE_THETA = 500000.0
NEG = -1.0e30


class Sem:
    def __init__(self, h, sid):
        self.h = h
        self.id = sid
        self.dead = False


class Eng:
    def __init__(self, name, h, compute):
        self.name = name
        self.h = h
        self.compute = compute
        self.sem = None
        self.count = 0
        self.waited = {}


class Buf:
    def __init__(self, ap, name="", const=False, untracked=False):
        self.ap = ap
        self.name = name
        self.const = const
        self.untracked = untracked
        self.w = None
        self.r = {}
        self.alias = []


def _exp(bufs):
    out = []
    for b in bufs:
        out.append(b)
        out.extend(b.alias)
    return out


class KB:
    def __init__(self, nc, planning):
        self.nc = nc
        self.planning = planning
        self.nsem = 0
        self.pe = Eng("pe", nc.tensor, True)
        self.act = Eng("act", nc.scalar, True)
        self.dve = Eng("dve", nc.vector, True)
        self.pool = Eng("pool", nc.gpsimd, True)
        self.sp = Eng("sp", nc.sync, False)
        self.engines = [self.pe, self.act, self.dve, self.pool, self.sp]
        self.dma_pool = []
        self.dma_rr = 0
        self.wplan = []
        self.wblocks = {}
        self.woff = 0
        self.wuse = 0
        self.wloaded = 0
        self.wseq = None
        self.ninstr = 0
        if not planning:
            for e in self.engines:
                if e.compute:
                    e.sem = self.new_sem(e.name)
            self.dma_pools = {}
            for q in (self.sp, self.pool, self.act):
                self.dma_pools[q.name] = [[self.new_sem("dma%s%d" % (q.name, i)), 0] for i in range({"sp": 12, "pool": 4, "act": 12}[q.name])]
            self.dma_rrs = {"sp": 0, "pool": 0, "act": 0}

    def new_sem(self, name):
        self.nsem += 1
        h = self.nc.alloc_semaphore(name="%s_%d" % (name, self.nsem))
        return Sem(h, self.nsem)

    def _wait(self, eng, tok):
        sem, val = tok
        if sem.dead:
            return
        if eng.waited.get(sem.id, 0) >= val:
            return
        eng.h.wait_ge(sem.h, val)
        eng.waited[sem.id] = val
        self.ninstr += 1

    def _deps(self, eng, reads, writes):
        reads = [b for b in _exp(reads) if not b.untracked]
        writes = [b for b in _exp(writes) if not b.untracked]
        skip_same = eng is self.pe
        for b in reads:
            if b.w is not None and not (skip_same and b.w[0] is eng.sem):
                self._wait(eng, b.w)
        for b in writes:
            if b.w is not None and not (skip_same and b.w[0] is eng.sem):
                self._wait(eng, b.w)
            for tok in b.r.values():
                if not (skip_same and tok[0] is eng.sem):
                    self._wait(eng, tok)

    def _mark(self, tok, reads, writes):
        reads = [b for b in reads if not b.untracked]
        writes = [b for b in writes if not b.untracked]
        for b in reads:
            if not b.const:
                b.r[tok[0].id] = tok
        for b in writes:
            b.w = tok
            b.r = {}

    def op(self, eng, fn, reads=(), writes=()):
        if self.planning:
            return
        self._deps(eng, reads, writes)
        ins = fn()
        eng.count += 1
        ins.then_inc(eng.sem.h, 1)
        self.ninstr += 1
        self._mark((eng.sem, eng.count), reads, writes)

    def dma(self, q, out_ap, in_ap, reads=(), writes=()):
        if self.planning:
            return
        self._deps(q, reads, writes)
        pl = self.dma_pools[q.name]
        ent = pl[self.dma_rrs[q.name]]
        self.dma_rrs[q.name] = (self.dma_rrs[q.name] + 1) % len(pl)
        if ent[1] > 0:
            self._wait(q, (ent[0], ent[1]))
        ent[1] += 16
        q.h.dma_start(out=out_ap, in_=in_ap).then_inc(ent[0].h, 16)
        self.ninstr += 1
        self._mark((ent[0], ent[1]), reads, writes)

    def barrier(self, new_epoch=True):
        if self.planning:
            return
        toks = [(e.sem, e.count) for e in self.engines if e.compute and e.count > 0]
        toks += [(ent[0], ent[1]) for pl in self.dma_pools.values() for ent in pl if ent[1] > 0]
        for e in self.engines:
            for tok in toks:
                if tok[0] is not e.sem:
                    self._wait(e, tok)
        if new_epoch:
            for e in self.engines:
                if e.compute:
                    e.sem.dead = True
                    e.sem = self.new_sem(e.name)
                    e.count = 0

    def maybe_epoch(self):
        if self.planning:
            return
        if max(e.count for e in self.engines if e.compute) > 9000:
            self.barrier(True)

    def wnext(self, key, n, hold=0):
        if self.planning:
            if key not in self.wblocks:
                self.wblocks[key] = (self.woff, n)
                self.woff += 128 * n
            self.wplan.append(key)
            return self.slots[0]
        u = self.wuse
        assert self.wseq[u] == key, (self.wseq[u], key)
        while self.wloaded < len(self.wseq) and self.wloaded < u + NSLOT - hold:
            k2 = self.wseq[self.wloaded]
            off, n2 = self.wblocks[k2]
            sl = self.slots[self.wloaded % NSLOT]
            src = self.wbf[off:off + 128 * n2].rearrange("(p n) -> p n", p=128)
            CH = 128 * 8192
            rb = self.wbf_bufs[off // CH:(off + 128 * n2 - 1) // CH + 1]
            self.dma(self.sp, sl.ap[:, 0:n2], src, reads=rb, writes=[sl])
            self.wloaded += 1
        self.wuse += 1
        return self.slots[u % NSLOT]


def _arr(ap3, name, n):
    return [Buf(ap3[:, i, :], "%s%d" % (name, i)) for i in range(n)]


class Prog:
    def __init__(self, S, layers, planning, plan=None, stop_after=None):
        self.S = S
        self.NT = S // T
        self.NB = S // 128
        self.layers = layers
        self.stop_after = stop_after
        import os
        self.start_barrier = os.environ.get("KB_START_BARRIER", "1") == "1"
        nc = bass.Bass("TRN2", target_bir_lowering=False)
        self.nc = nc
        K = KB(nc, planning)
        self.K = K
        if plan is not None:
            K.wseq = plan["wseq"]
            K.wblocks = plan["wblocks"]
            self.wtotal = plan["wtotal"]
        else:
            self.wtotal = 128 * SLOT
        self.alloc()
        self.emit()
        if planning:
            tot = K.woff
            pad = (-tot) % (128 * 8192)
            self.plan = {"wseq": K.wplan, "wblocks": K.wblocks, "wtotal": tot + pad}

    def alloc(self):
        nc, K, S = self.nc, self.K, self.S
        dt = nc.dram_tensor
        self.xin = dt("xT", [D, S], F32, kind="ExternalInput").ap()
        self.wsrc = dt("wsrc", [self.wtotal], F32, kind="ExternalInput").ap()
        self.prm = dt("prm", [128, 256], F32, kind="ExternalInput").ap()
        self.tabs = dt("tabs", [4, 128, S], F32, kind="ExternalInput").ap()
        self.cmb_in = dt("cmb_in", [128, 4 * 512 + 128], BF16, kind="ExternalInput").ap()
        self.cneg_in = dt("cneg_in", [128, 128], F32, kind="ExternalInput").ap()
        self.cnT_in = dt("cnT_in", [128, 4 * T], F32, kind="ExternalInput").ap()
        self.xout = dt("outT", [D, S], F32, kind="ExternalOutput").ap()
        K.wbf = dt("wbf", [self.wtotal], BF16).ap()
        K.wbf_bufs = [Buf(K.wbf, "wbf%d" % i) for i in range(self.wtotal // (128 * 8192))]
        self.xs = [dt("xs%d" % i, [D, S], F32).ap() for i in range(2)]
        self.xs_buf = [Buf(self.xs[i], "xs%d" % i, untracked=True) for i in range(2)]
        self.QT = dt("QT", [H, 128, S], BF16).ap()
        self.KT = dt("KT", [H, 128, S], BF16).ap()
        self.VS = dt("VS", [H, 128, self.NB, 64], BF16).ap()
        self.QI = dt("QI", [8, 64, S], BF16).ap()
        self.KI = dt("KI", [64, S], BF16).ap()
        self.WI = dt("WI", [S, 8], F32).ap()
        self.QT_buf, self.KT_buf, self.VS_buf = Buf(self.QT, "QT", untracked=True), Buf(self.KT, "KT", untracked=True), Buf(self.VS, "VS", untracked=True)
        self.QI_buf, self.KI_buf, self.WI_buf = Buf(self.QI, "QI", untracked=True), Buf(self.KI, "KI", untracked=True), Buf(self.WI, "WI", untracked=True)
        self.xin_buf = Buf(self.xin, "xin", const=True)
        self.xout_buf = Buf(self.xout, "xout", untracked=True)

        sb = lambda name, shape, dtype: nc.alloc_sbuf_tensor(name, shape, dtype).ap()
        ring = sb("ring", [128, NSLOT, SLOT], BF16)
        K.slots = [Buf(ring[:, i, :], "slot%d" % i) for i in range(NSLOT)]
        xt_ = sb("x", [128, KC, T], F32)
        self.x = _arr(xt_, "x", KC)
        self.x_full = xt_.rearrange("p a t -> p (a t)")
        self.xb = _arr(sb("xb", [128, KC, T], BF16), "xb", KC)
        zt = sb("z", [128, KC, T], F32)
        self.z = _arr(zt, "z", KC)
        self.z_full = zt.rearrange("p a t -> p (a t)")
        gt = sb("gT", [128, FC, T], BF16)
        self.gT = _arr(gt, "gT", FC)
        self.gT_full = gt.rearrange("p a t -> p (a t)")
        self.tmp = _arr(sb("tmp", [128, 4, T], F32), "tmp", 4)
        self.stat = _arr(sb("stat", [128, 4, T], F32), "stat", 4)
        self.prm_sb = Buf(sb("prm_sb", [128, 256], F32), "prm_sb")
        self.prm2_sb = Buf(sb("prm2_sb", [128, 256], F32), "prm2_sb")
        self.ones32 = Buf(sb("ones32", [128, 128], F32), "ones32")
        self.cmb = Buf(sb("cmb", [128, 4 * 512 + 128], BF16), "cmb")
        self.cneg = Buf(sb("cneg", [128, 128], F32), "cneg")
        self.psum = [Buf(nc.alloc_psum_tensor("ps%d" % i, [128, 512], F32).ap(), "ps%d" % i) for i in range(8)]

    def mm(self, ps, pairs, reads, M=128, N=T, pbase=0):
        K = self.K
        pe = K.pe.h
        out = ps.ap[pbase:pbase + M, 0:N]

        def f():
            ins = None
            n = len(pairs)
            for i, (l, r) in enumerate(pairs):
                ins = pe.matmul(out, l, r, start=(i == 0), stop=(i == n - 1))
            K.ninstr += n - 1
            return ins
        K.op(K.pe, f, reads=reads, writes=[ps])

    def prmcol(self, c, n=128):
        return self.prm_sb.ap[0:n, c:c + 1]

    def ffn(self, L, which):
        K = self.K
        act, dve = K.act.h, K.dve.h
        xb, gT, z, x = self.xb, self.gT, self.z, self.x
        for cp in range(FC // 2):
            slot = K.wnext(("w13", L, which, cp), 4096)
            for c2 in range(2):
                c = cp * 2 + c2
                psg, psu = self.psum[(c % 2) * 2], self.psum[(c % 2) * 2 + 1]
                for gu, ps in ((0, psg), (1, psu)):
                    base = (c2 * 2 + gu) * KC
                    self.mm(ps, [(slot.ap[:, (base + k) * 128:(base + k + 1) * 128], xb[k].ap) for k in range(KC)],
                            reads=[slot] + xb)
                sg = self.tmp[c % 2]
                K.op(K.act, lambda: act.activation(sg.ap, psg.ap, AF.Silu), reads=[psg], writes=[sg])
                K.op(K.dve, lambda: dve.tensor_tensor(gT[c].ap, psu.ap, sg.ap, ALU.mult), reads=[psu, sg], writes=[gT[c]])
        for m in range(KC):
            slot = K.wnext(("w2", L, which, m), FC * 128)
            ps = self.psum[4 + m % 2]
            self.mm(ps, [(slot.ap[:, c * 128:(c + 1) * 128], gT[c].ap) for c in range(FC)], reads=[slot] + gT)
            K.op(K.dve, lambda: dve.scalar_tensor_tensor(z[m].ap, ps.ap, 0.5, x[m].ap, ALU.mult, ALU.add),
                 reads=[ps, x[m]], writes=[z[m]])

    def layernorm(self, gcol, bcol, scale_after):
        K = self.K
        act, dve, pool, pe = K.act.h, K.dve.h, K.pool.h, K.pe.h
        z, x, xb = self.z, self.x, self.xb
        ps_s, ps_q = self.psum[4], self.psum[5]
        mean, var, std, rstd = self.stat
        self.mm(ps_s, [(self.ones32.ap, z[m].ap) for m in range(KC)], reads=z + [self.ones32])
        for m in range(KC):
            sq = self.tmp[2 + m % 2]
            K.op(K.act, lambda: act.activation(sq.ap, z[m].ap, AF.Square), reads=[z[m]], writes=[sq])
            K.op(K.pe, lambda: pe.matmul(ps_q.ap, self.ones32.ap, sq.ap, start=(m == 0), stop=(m == KC - 1)),
                 reads=[sq, self.ones32], writes=[ps_q])
        K.op(K.dve, lambda: dve.tensor_scalar(mean.ap, ps_s.ap, 1.0 / D, None, ALU.mult), reads=[ps_s], writes=[mean])
        K.op(K.pool, lambda: pool.tensor_tensor(var.ap, mean.ap, mean.ap, ALU.mult), reads=[mean], writes=[var])
        K.op(K.dve, lambda: dve.scalar_tensor_tensor(var.ap, ps_q.ap, 1.0 / D, var.ap, ALU.mult, ALU.subtract),
             reads=[ps_q, var], writes=[var])
        K.op(K.dve, lambda: dve.tensor_scalar(var.ap, var.ap, LN_EPS, None, ALU.add), reads=[var], writes=[var])
        K.op(K.act, lambda: act.activation(std.ap, var.ap, AF.Sqrt), reads=[var], writes=[std])
        K.op(K.dve, lambda: dve.reciprocal(rstd.ap, std.ap), reads=[std], writes=[rstd])
        K.op(K.dve, lambda: dve.scalar_tensor_tensor(var.ap, mean.ap, -1.0, rstd.ap, ALU.mult, ALU.mult), reads=[mean, rstd], writes=[var])
        pb = self.prm2_sb if scale_after else self.prm_sb
        for m in range(KC):
            t = self.tmp[m % 2]
            K.op(K.dve, lambda: dve.tensor_tensor(t.ap, z[m].ap, rstd.ap, ALU.mult), reads=[z[m], rstd], writes=[t])
            K.op(K.pool, lambda: pool.tensor_tensor(t.ap, t.ap, var.ap, ALU.add), reads=[t, var], writes=[t])
            K.op(K.act, lambda: act.activation(x[m].ap, t.ap, AF.Identity, bias=pb.ap[:, bcol + m:bcol + m + 1], scale=pb.ap[:, gcol + m:gcol + m + 1]),
                 reads=[t, pb], writes=[x[m]])
            K.op(K.pool, lambda: pool.tensor_scalar(xb[m].ap, t.ap, self.prmcol(gcol + m), self.prmcol(bcol + m), ALU.mult, ALU.add),
                 reads=[t, self.prm_sb], writes=[xb[m]])

    def load_x(self, src, src_buf, j, want_xb, q=None, do_dma=True, do_prep=True):
        K = self.K
        pool = K.pool.h
        q = q or K.sp
        if do_dma:
            for m in range(KC):
                K.dma(q, self.x[m].ap, src[m * 128:(m + 1) * 128, j * T:(j + 1) * T], reads=[src_buf], writes=[self.x[m]])
        if do_prep:
            for m in range(KC):
                if want_xb:
                    K.op(K.pool, lambda: pool.tensor_copy(self.xb[m].ap, self.x[m].ap), reads=[self.x[m]], writes=[self.xb[m]])
                K.op(K.pool, lambda: pool.tensor_scalar(self.x[m].ap, self.x[m].ap, ALPHA, 0.0, ALU.mult, ALU.add),
                     reads=[self.x[m]], writes=[self.x[m]])

    def store_x(self, dst, dst_buf, j):
        K = self.K
        for m in range(KC):
            K.dma(K.act, dst[m * 128:(m + 1) * 128, j * T:(j + 1) * T], self.x[m].ap, reads=[self.x[m]], writes=[dst_buf])

    def proj_group(self, slot, off_main, off_swap, rhs, nk, M, ps_main, ps_swap):
        self.mm(ps_main, [(slot.ap[:, off_main + k * M: off_main + (k + 1) * M], rhs[k].ap) for k in range(nk)],
                reads=[slot] + rhs[:nk], M=M)
        if off_swap is not None:
            self.mm(ps_swap, [(slot.ap[:, off_swap + k * M: off_swap + (k + 1) * M], rhs[k].ap) for k in range(nk)],
                    reads=[slot] + rhs[:nk], M=M)

    def rope_evac(self, ps_main, ps_swap, M, tC, tS, dst, par=0):
        K = self.K
        dve, pool = K.dve.h, K.pool.h
        t1, t2 = self.tmp[(par % 2) * 2], self.tmp[(par % 2) * 2 + 1]
        K.op(K.dve, lambda: dve.tensor_tensor(t1.ap[0:M], ps_main.ap[0:M], tC.ap[0:M], ALU.mult), reads=[ps_main, tC], writes=[t1])
        K.op(K.dve, lambda: dve.tensor_tensor(t2.ap[0:M], ps_swap.ap[0:M], tS.ap[0:M], ALU.mult), reads=[ps_swap, tS], writes=[t2])
        K.op(K.pool, lambda: pool.tensor_tensor(dst.ap[0:M], t1.ap[0:M], t2.ap[0:M], ALU.add), reads=[t1, t2], writes=[dst])

    def vproj(self, L, j, key, src, nk, bufs):
        K = self.K
        act = K.act.h
        if nk * 1024 > SLOT:
            s0 = K.wnext((key, L, 0), nk * 512)
            s1 = K.wnext((key, L, 1), nk * 512, hold=1)
            wap = lambda k, half: (s0, s1)[half].ap[:, k * 512:(k + 1) * 512]
            sl = [s0, s1]
        else:
            s0 = K.wnext((key, L), nk * 1024)
            wap = lambda k, half: s0.ap[:, k * 1024 + half * 512: k * 1024 + (half + 1) * 512]
            sl = [s0]
        for tb in range(4):
            vst = bufs["vst"][tb % 2]
            for half in range(2):
                ps = self.psum[half]
                self.mm(ps, [(src[k].ap[:, tb * 128:(tb + 1) * 128], wap(k, half)) for k in range(nk)], reads=sl + src[:nk])
                K.op(K.act, lambda: act.copy(vst.ap[:, half * 512:(half + 1) * 512], ps.ap), reads=[ps], writes=[vst])
            nb = 4 * j + tb
            K.dma(K.act, self.VS[:, :, nb, :].rearrange("h p d -> p h d"), vst.ap.rearrange("p (h d) -> p h d", d=64),
                  reads=[vst], writes=[self.VS_buf])

    def store_heads(self, dst, dst_buf, stage, i, j, rows=64):
        K = self.K
        K.dma(K.act, dst[2 * i, 0:rows, j * T:(j + 1) * T], stage.ap[0:rows], reads=[stage], writes=[dst_buf])
        K.dma(K.act, dst[2 * i + 1, 0:rows, j * T:(j + 1) * T], stage.ap[64:64 + rows], reads=[stage], writes=[dst_buf])

    def phase_a(self, L, src, src_buf):
        from contextlib import ExitStack
        K, nc = self.K, self.nc
        act, dve, pool = K.act.h, K.dve.h, K.pool.h
        kind = L % 3
        with ExitStack() as es:
            sb = lambda name, shape, dtype: es.enter_context(nc.sbuf_tensor(name + "_a%d" % L, shape, dtype)).ap()
            bufs = {}
            rt = _arr(sb("rt", [128, 2, T], F32), "rt", 2)
            stage = _arr(sb("stage", [128, 2, T], BF16), "stage", 2)
            bufs["vst"] = _arr(sb("vst", [128, 2, 1024], BF16), "vst", 2)
            if kind == 0:
                wst = _arr(sb("wst", [128, 2, 8], F32), "wst", 2)
            if kind == 1:
                cum = _arr(sb("cum", [16, 4, T], F32), "cum", 4)
                cons = Buf(sb("cons", [16, T], F32), "cons")
                cst = Buf(sb("cst", [16, 6, T], BF16), "cst")
                cst1 = Buf(sb("cst1", [16, 3, T], BF16), "cst1")
                carry = Buf(sb("carry", [16, 2], F32), "carry")
                negb = Buf(sb("negb", [16, 1], F32), "negb")
                K.op(K.pool, lambda: pool.memset(cons.ap, 1.0), writes=[cons])
                K.op(K.pool, lambda: pool.memset(cst1.ap, 1.0), writes=[cst1])
                K.op(K.pool, lambda: pool.memset(carry.ap, 0.0), writes=[carry])
                K.op(K.dve, lambda: dve.tensor_scalar(negb.ap, self.prm_sb.ap[0:16, 192:193], -1.0, None, ALU.mult),
                     reads=[self.prm_sb], writes=[negb])
            if kind == 2:
                lat = _arr(sb("lat", [128, 5, T], F32), "lat", 5)
                latn = _arr(sb("latn", [128, 5, T], BF16), "latn", 5)

            x2t = sb("x2", [128, KC, T], F32)
            xb2t = sb("xb2", [128, KC, T], BF16)
            z2t = sb("z2", [128, KC, T], F32)
            sets = [(self.x, self.xb, self.z), (_arr(x2t, "x2_", KC), _arr(xb2t, "xb2_", KC), _arr(z2t, "z2_", KC))]

            def set_par(p):
                self.x, self.xb, self.z = sets[p % 2]

            def proj(j):
                cols = slice(j * T, (j + 1) * T)
                xb = self.xb
                if kind in (0, 2):
                    tb0 = 0 if kind == 0 else 2
                    for t in range(2):
                        K.dma(K.sp, rt[t].ap, self.tabs[tb0 + t, :, cols], reads=[], writes=[rt[t]])
                if kind == 0:
                    for g in range(21):
                        if g % 2 == 0:
                            slot = K.wnext(("pj", L, g // 2), 4096)
                        off = (g % 2) * 2048
                        pm, psw = self.psum[(g % 3) * 2], self.psum[(g % 3) * 2 + 1]
                        self.proj_group(slot, off, off + 1024, xb, KC, 128, pm, psw)
                        st = stage[g % 2]
                        self.rope_evac(pm, psw, 128, rt[0], rt[1], st, par=g)
                        if g < 8:
                            self.store_heads(self.QT, self.QT_buf, st, g, j)
                        elif g < 16:
                            self.store_heads(self.KT, self.KT_buf, st, g - 8, j)
                        elif g < 20:
                            self.store_heads(self.QI, self.QI_buf, st, g - 16, j)
                        else:
                            K.dma(K.act, self.KI[:, cols], st.ap[0:64], reads=[st], writes=[self.KI_buf])
                    self.vproj(L, j, "pv", xb, KC, bufs)
                    slot = K.wnext(("pw", L), 64)
                    for tb in range(4):
                        ps = self.psum[2]
                        self.mm(ps, [(xb[k].ap[:, tb * 128:(tb + 1) * 128], slot.ap[:, k * 8:(k + 1) * 8]) for k in range(KC)],
                                reads=[slot] + xb, N=8)
                        w = wst[tb % 2]
                        K.op(K.dve, lambda: dve.tensor_copy(w.ap, ps.ap[:, 0:8]), reads=[ps], writes=[w])
                        K.dma(K.act, self.WI[j * T + tb * 128: j * T + (tb + 1) * 128, :], w.ap, reads=[w], writes=[self.WI_buf])
                elif kind == 1:
                    for g in range(16):
                        if g % 4 == 0:
                            slot = K.wnext(("pj", L, g // 4), 4096)
                        pm = self.psum[g % 4]
                        self.proj_group(slot, (g % 4) * 1024, None, xb, KC, 128, pm, None)
                        st = stage[g % 2]
                        K.op(K.act, lambda: act.copy(st.ap, pm.ap), reads=[pm], writes=[st])
                        if g < 8:
                            self.store_heads(self.QT, self.QT_buf, st, g, j)
                        else:
                            self.store_heads(self.KT, self.KT_buf, st, g - 8, j)
                    slot = K.wnext(("pf", L), 128)
                    pm = self.psum[0]
                    self.proj_group(slot, 0, None, xb, KC, 16, pm, None)
                    e, l1, r1, r2 = cum
                    K.op(K.act, lambda: act.activation(e.ap, pm.ap[0:16], AF.Exp, bias=negb.ap, scale=-1.0), reads=[pm, negb], writes=[e])
                    K.op(K.act, lambda: act.activation(l1.ap, e.ap, AF.Ln, bias=1.0, scale=1.0), reads=[e], writes=[l1])
                    K.op(K.dve, lambda: dve.tensor_tensor_scan(e.ap, cons.ap, l1.ap, carry.ap[:, 0:1], ALU.mult, ALU.add),
                         reads=[cons, l1, carry], writes=[e])
                    K.op(K.dve, lambda: dve.tensor_copy(carry.ap[:, 0:1], e.ap[:, T - 1:T]), reads=[e], writes=[carry])
                    K.op(K.dve, lambda: dve.tensor_scalar(l1.ap, e.ap, 8.0, None, ALU.mult), reads=[e], writes=[l1])
                    K.op(K.dve, lambda: dve.tensor_copy(cst.ap[:, 3, :], l1.ap), reads=[l1], writes=[cst])
                    K.op(K.dve, lambda: dve.tensor_tensor(r1.ap, l1.ap, cst.ap[:, 3, :], ALU.subtract), reads=[l1, cst], writes=[r1])
                    K.op(K.dve, lambda: dve.tensor_copy(cst.ap[:, 4, :], r1.ap), reads=[r1], writes=[cst])
                    K.op(K.dve, lambda: dve.tensor_tensor(r2.ap, r1.ap, cst.ap[:, 4, :], ALU.subtract), reads=[r1, cst], writes=[r2])
                    K.op(K.dve, lambda: dve.tensor_copy(cst.ap[:, 5, :], r2.ap), reads=[r2], writes=[cst])
                    K.op(K.dve, lambda: dve.tensor_scalar(cst.ap[:, 0:3, :], cst.ap[:, 3:6, :], -1.0, None, ALU.mult), reads=[cst], writes=[cst])
                    K.dma(K.act, self.QT[:, 64:67, cols], cst.ap[:, 0:3, :], reads=[cst], writes=[self.QT_buf])
                    K.dma(K.act, self.KT[:, 67:70, cols], cst.ap[:, 3:6, :], reads=[cst], writes=[self.KT_buf])
                    K.dma(K.act, self.QT[:, 67:70, cols], cst1.ap, reads=[cst1], writes=[self.QT_buf])
                    K.dma(K.act, self.KT[:, 64:67, cols], cst1.ap, reads=[cst1], writes=[self.KT_buf])
                    self.vproj(L, j, "pv", xb, KC, bufs)
                else:
                    for g in range(5):
                        if g % 4 == 0:
                            slot = K.wnext(("pj", L, g // 4), 4096)
                        pm = self.psum[g % 4]
                        self.proj_group(slot, (g % 4) * 1024, None, xb, KC, 128, pm, None)
                        K.op(K.act, lambda: act.copy(lat[g].ap, pm.ap), reads=[pm], writes=[lat[g]])
                    slot = K.wnext(("pkr", L), 2 * KC * 96)
                    pm, psw = self.psum[0], self.psum[1]
                    self.proj_group(slot, 0, KC * 96, xb, KC, 96, pm, psw)
                    st = stage[0]
                    self.rope_evac(pm, psw, 96, rt[0], rt[1], st)
                    for h in range(H):
                        K.dma(K.act, self.KT[h, 64:96, cols], st.ap[64:96], reads=[st], writes=[self.KT_buf])
                    for (m0, nm, gc) in ((0, 3, 193), (3, 2, 196)):
                        ps_q = self.psum[4]
                        for mi in range(nm):
                            sq = self.tmp[2 + mi % 2]
                            K.op(K.act, lambda: act.activation(sq.ap, lat[m0 + mi].ap, AF.Square), reads=[lat[m0 + mi]], writes=[sq])
                            K.op(K.pe, lambda: K.pe.h.matmul(ps_q.ap, self.ones32.ap, sq.ap, start=(mi == 0), stop=(mi == nm - 1)),
                                 reads=[sq, self.ones32], writes=[ps_q])
                        var, std, rstd = self.stat[1], self.stat[2], self.stat[3]
                        K.op(K.dve, lambda: dve.tensor_scalar(var.ap, ps_q.ap, 1.0 / (nm * 128), RMS_EPS, ALU.mult, ALU.add),
                             reads=[ps_q], writes=[var])
                        K.op(K.act, lambda: act.activation(std.ap, var.ap, AF.Sqrt), reads=[var], writes=[std])
                        K.op(K.dve, lambda: dve.reciprocal(rstd.ap, std.ap), reads=[std], writes=[rstd])
                        for mi in range(nm):
                            K.op(K.dve, lambda: dve.scalar_tensor_tensor(latn[m0 + mi].ap, lat[m0 + mi].ap, self.prmcol(gc + mi), rstd.ap,
                                                                         ALU.mult, ALU.mult),
                                 reads=[lat[m0 + mi], rstd, self.prm_sb], writes=[latn[m0 + mi]])
                    for h in range(H):
                        if h % 4 == 0:
                            slot = K.wnext(("uq", L, h // 4), 4 * 576)
                        pm, psw = self.psum[(h % 2) * 2], self.psum[(h % 2) * 2 + 1]
                        off = (h % 4) * 576
                        self.proj_group(slot, off, off + 288, latn[0:3], 3, 96, pm, psw)
                        st = stage[h % 2]
                        self.rope_evac(pm, psw, 96, rt[0], rt[1], st, par=h)
                        K.dma(K.act, self.QT[h, 0:96, cols], st.ap[0:96], reads=[st], writes=[self.QT_buf])
                    slot = K.wnext(("uk", L), 8 * 256)
                    for i in range(8):
                        pm = self.psum[i % 4]
                        self.proj_group(slot, i * 256, None, latn[3:5], 2, 128, pm, None)
                        st = stage[i % 2]
                        K.op(K.act, lambda: act.copy(st.ap, pm.ap), reads=[pm], writes=[st])
                        self.store_heads(self.KT, self.KT_buf, st, i, j)
                    self.vproj(L, j, "uv", latn[3:5], 2, bufs)

            def ld(j):
                set_par(j)
                self.load_x(src, src_buf, j, True, q=K.act, do_prep=False)

            def prep(j):
                set_par(j)
                self.load_x(src, src_buf, j, True, do_dma=False)

            def ffn_front(j):
                set_par(j)
                self.ffn(L, 0)

            def ffn_ln_back(j):
                set_par(j)
                self.layernorm(gcol=L * 24 + 0, bcol=96 + L * 24 + 0, scale_after=False)
                self.store_x(self.xs[0], self.xs_buf[0], j)

            NT = self.NT
            ld(0)
            prep(0)
            ffn_front(0)
            ffn_ln_back(0)
            if NT > 1:
                ld(1)
                prep(1)
            for j in range(NT):
                K.maybe_epoch()
                if j + 1 < NT:
                    ffn_front(j + 1)
                if j + 2 < NT:
                    ld(j + 2)
                set_par(j)
                proj(j)
                if j + 2 < NT:
                    prep(j + 2)
                if j + 1 < NT:
                    ffn_ln_back(j + 1)
            set_par(0)
            K.barrier(False)

    def attention(self, j, dk, scale, masked, ab):
        K = self.K
        act, dve, pool, pe = K.act.h, K.dve.h, K.pool.h, K.pe.h
        nk = (j + 1) * T
        nb = 4 * (j + 1)
        cols = slice(j * T, (j + 1) * T)
        ktb, vev, vod, qtb, pT, rec, bcs = ab["ktb"], ab["vev"], ab["vod"], ab["qtb"], ab["pT"], ab["rec"], ab["bcs"]
        pend = None
        for h in range(H):
            odd = h % 2
            kt = ktb[h % 2]
            qt = qtb[h % 2]
            vb = vod if odd else vev
            K.dma(K.sp, kt.ap[0:dk, 0:nk], self.KT[h, 0:dk, 0:nk], reads=[self.KT_buf], writes=[kt])
            K.dma(K.sp, qt.ap[0:dk, :], self.QT[h, 0:dk, cols], reads=[self.QT_buf], writes=[qt])
            vdst = vb.ap[:, 0:nb, 64:128] if odd else vb.ap[:, 0:nb, 0:64]
            K.dma(K.sp, vdst, self.VS[h, :, 0:nb, :], reads=[self.VS_buf], writes=[vb])
            ps_o = self.psum[4 + odd]
            SB = 4

            def smm(kb, kt=kt, qt=qt):
                self.mm(self.psum[kb % SB], [(kt.ap[0:dk, kb * 128:(kb + 1) * 128], qt.ap[0:dk, :])], reads=[kt, qt])
            for kb in range(min(SB, nb)):
                smm(kb)
            if pend is not None:
                pend()
            for kb in range(nb):
                ps_s = self.psum[kb % SB]
                p = pT[kb % 4]
                if masked is None and kb >= 4 * j:
                    a = kb - 4 * j
                    tt = self.tmp[kb % 2]
                    K.op(K.dve, lambda: dve.tensor_tensor(tt.ap, ps_s.ap, ab["cnT"].ap[:, a * 512:(a + 1) * 512], ALU.add),
                         reads=[ps_s, ab["cnT"]], writes=[tt])
                    K.op(K.act, lambda: act.activation(p.ap, tt.ap, AF.Exp, scale=scale), reads=[tt], writes=[p])
                else:
                    K.op(K.act, lambda: act.activation(p.ap, ps_s.ap, AF.Exp, scale=scale), reads=[ps_s], writes=[p])
                if masked is not None:
                    mk = masked.ap[:, kb, :]
                    if kb % 2 == 0:
                        K.op(K.dve, lambda: dve.tensor_tensor(p.ap, p.ap, mk, ALU.mult), reads=[p, masked], writes=[p])
                    else:
                        K.op(K.pool, lambda: pool.tensor_tensor(p.ap, p.ap, mk, ALU.mult), reads=[p, masked], writes=[p])
                if odd:
                    lhsT, M = vb.ap[:, kb, 0:128], 128
                else:
                    lhsT, M = vb.ap[:, kb, 0:65], 65
                K.op(K.pe, lambda: pe.matmul(ps_o.ap[0:M, :], lhsT, p.ap, start=(kb == 0), stop=(kb == nb - 1)),
                     reads=[vb, p], writes=[ps_o])
                if kb + SB < nb:
                    smm(kb + SB)

            def epilogue(h=h, odd=odd, ps_o=ps_o):
                dr = 0 if odd else 64
                ps_b = self.psum[6]
                K.op(K.dve, lambda: dve.reciprocal(rec.ap[dr:dr + 1, :], ps_o.ap[dr:dr + 1, :]), reads=[ps_o], writes=[rec])
                K.op(K.pe, lambda: pe.matmul(ps_b.ap, self.ones32.ap[dr:dr + 1, :], rec.ap[dr:dr + 1, :], start=True, stop=True),
                     reads=[rec, self.ones32], writes=[ps_b])
                bc = bcs[h % 2]
                ob = 64 if odd else 0
                K.op(K.act, lambda: act.copy(bc.ap[ob:ob + 64], ps_b.ap[ob:ob + 64]), reads=[ps_b], writes=[bc])
                K.op(K.dve, lambda: dve.tensor_tensor(self.OT[h // 2].ap[ob:ob + 64], ps_o.ap[ob:ob + 64], bc.ap[ob:ob + 64], ALU.mult),
                     reads=[ps_o, bc], writes=[self.OT[h // 2]])
            pend = epilogue
        pend()

    def dsa_topk(self, j, ab):
        K = self.K
        act, dve, pool, pe = K.act.h, K.dve.h, K.pool.h, K.pe.h
        maskT, mk, qib, kib, wib, bis, bisg = (ab[k] for k in ("maskT", "mk", "qib", "kib", "wib", "bis", "bisg"))
        scs = [ab["sc"], ab["sc1"]]
        nkt = (j + 1) * T
        nbt = 4 * (j + 1)
        K.dma(K.sp, kib.ap[:, 0:nkt], self.KI[:, 0:nkt], reads=[self.KI_buf], writes=[kib])
        NIT = 20
        blocks = []
        for qs in range(4):
            i = 4 * j + qs
            qc = slice(qs * 128, (qs + 1) * 128)
            if i + 1 < nbt:
                K.op(K.pool, lambda: pool.memset(maskT.ap[:, i + 1:nbt, qc], 0.0), writes=[maskT])
            if i < 2:
                if i > 0:
                    K.op(K.pool, lambda: pool.memset(maskT.ap[:, 0:i, qc], 1.0), writes=[maskT])
                K.op(K.pool, lambda: pool.tensor_copy(maskT.ap[:, i, qc], self.cmb.ap[:, 0:128]), reads=[self.cmb], writes=[maskT])
            else:
                blocks.append(qs)

        def scores(qs):
            i = 4 * j + qs
            nk = 128 * (i + 1)
            sc = scs[qs % 2]
            K.dma(K.sp, qib.ap, self.QI[:, :, i * 128:(i + 1) * 128].rearrange("h d t -> d h t"), reads=[self.QI_buf], writes=[qib])
            K.dma(K.sp, wib.ap, self.WI[i * 128:(i + 1) * 128, :], reads=[self.WI_buf], writes=[wib])
            for g in range((nk + 511) // 512):
                n = min(512, nk - g * 512)
                scg = sc.ap[:, g * 512:g * 512 + n]
                for h in range(8):
                    ps = self.psum[2 + h % 2]
                    self.mm(ps, [(qib.ap[:, h, :], kib.ap[:, g * 512:g * 512 + n])], reads=[qib, kib], N=n)
                    r = self.tmp[h % 2]
                    K.op(K.act, lambda: act.activation(r.ap[:, 0:n], ps.ap[:, 0:n], AF.Relu), reads=[ps], writes=[r])
                    if h == 0:
                        K.op(K.pool, lambda: pool.tensor_scalar(scg, r.ap[:, 0:n], wib.ap[:, 0:1], 0.0, ALU.mult, ALU.add),
                             reads=[r, wib], writes=[sc])
                    else:
                        tp = self.tmp[2 + h % 2]
                        K.op(K.pool, lambda: pool.tensor_scalar(tp.ap[:, 0:n], r.ap[:, 0:n], wib.ap[:, h:h + 1], 0.0, ALU.mult, ALU.add),
                             reads=[r, wib], writes=[tp])
                        K.op(K.pool, lambda: pool.tensor_tensor(scg, scg, tp.ap[:, 0:n], ALU.add), reads=[sc, tp], writes=[sc])

        def select(qs):
            i = 4 * j + qs
            nk = 128 * (i + 1)
            qc = slice(qs * 128, (qs + 1) * 128)
            sc = scs[qs % 2]
            scn = sc.ap[:, 0:nk]
            lo, w0, mid, cnt, mx = (bis.ap[:, c:c + 1] for c in range(5))
            K.op(K.dve, lambda: dve.tensor_reduce(mx, scn, AX.X, ALU.max), reads=[sc], writes=[bis])
            K.op(K.dve, lambda: dve.tensor_reduce(lo, scn, AX.X, ALU.min), reads=[sc], writes=[bis])
            K.op(K.dve, lambda: dve.tensor_tensor(sc.ap[:, i * 128:(i + 1) * 128], sc.ap[:, i * 128:(i + 1) * 128], self.cneg.ap, ALU.add),
                 reads=[sc, self.cneg], writes=[sc])
            K.op(K.dve, lambda: dve.tensor_tensor(w0, mx, lo, ALU.subtract), reads=[bis], writes=[bis])
            K.op(K.dve, lambda: dve.tensor_scalar(w0, w0, 1.001, 1e-20, ALU.mult, ALU.add), reads=[bis], writes=[bis])
            for it in range(NIT):
                K.op(K.dve, lambda: dve.tensor_scalar(mid, w0, 2.0 ** (-(it + 1)), lo, ALU.mult, ALU.add), reads=[bis], writes=[bis])
                K.op(K.dve, lambda: dve.tensor_scalar(mk.ap[:, 0:nk], scn, mid, None, ALU.is_ge, ALU.add, accum_out=cnt),
                     reads=[sc, bis], writes=[mk, bis])
                K.op(K.dve, lambda: dve.tensor_scalar(bisg.ap[:, 0:1], cnt, float(TOPK), None, ALU.is_ge), reads=[bis], writes=[bisg])
                K.op(K.dve, lambda: dve.copy_predicated(lo, bisg.ap[:, 0:1], mid), reads=[bis, bisg], writes=[bis])
            K.op(K.dve, lambda: dve.tensor_scalar(mk.ap[:, 0:nk], scn, lo, None, ALU.is_ge), reads=[sc, bis], writes=[mk])
            pst = self.psum[7]
            pstb = pst.ap.bitcast(BF16)
            for kb0 in range(0, i + 1, 4):
                nbk = min(4, i + 1 - kb0)

                def tr():
                    ins = None
                    for t in range(nbk):
                        ins = pe.transpose(pstb[:, t * 128:(t + 1) * 128], mk.ap[:, (kb0 + t) * 128:(kb0 + t + 1) * 128],
                                           self.cmb.ap[:, 2048:2176])
                    return ins
                K.op(K.pe, tr, reads=[mk, self.cmb], writes=[pst])
                K.op(K.act, lambda: act.copy(maskT.ap[:, kb0:kb0 + nbk, qc], pstb[:, 0:nbk * 128].rearrange("p (a t) -> p a t", t=128)),
                     reads=[pst], writes=[maskT])

        if blocks:
            scores(blocks[0])
        for bi, qs in enumerate(blocks):
            if bi + 1 < len(blocks):
                scores(blocks[bi + 1])
            select(qs)

    def phase_b(self, L, dst, dst_buf, last):
        from contextlib import ExitStack
        K, nc = self.K, self.nc
        act, dve, pool = K.act.h, K.dve.h, K.pool.h
        kind = L % 3
        S = self.S
        dk, scale = ((64, 64 ** -0.5), (70, 64 ** -0.5), (96, 96 ** -0.5))[kind]
        with ExitStack() as es:
            sb = lambda name, shape, dtype: es.enter_context(nc.sbuf_tensor(name + "_b%d" % L, shape, dtype)).ap()
            ab = {}
            ab["ktb"] = _arr(sb("ktb", [128, 2, S], BF16), "ktb", 2)
            ab["vev"] = Buf(sb("vev", [128, self.NB, 80], BF16), "vev")
            ab["vod"] = Buf(sb("vod", [128, self.NB, 128], BF16), "vod")
            ab["qtb"] = _arr(sb("qtb", [128, 2, T], BF16), "qtb", 2)
            ab["pT"] = _arr(sb("pT", [128, 4, T], BF16), "pT", 4)
            ab["rec"] = self.stat[0]
            ab["bcs"] = [self.stat[1], self.stat[2]]
            self.OT = _arr(sb("OT", [128, KC, T], BF16), "OT", KC)
            K.op(K.pool, lambda: pool.memset(ab["vev"].ap[:, :, 64:65], 1.0), writes=[ab["vev"]])
            K.op(K.pool, lambda: pool.memset(ab["vod"].ap[:, :, 0:64], 0.0), writes=[ab["vod"]])
            K.op(K.pool, lambda: pool.memset(ab["vod"].ap[:, :, 0:1], 1.0), writes=[ab["vod"]])
            if kind != 0:
                ab["cnT"] = Buf(sb("cnT", [128, 4 * T], F32), "cnT")
                K.dma(K.sp, ab["cnT"].ap, self.cnT_in, writes=[ab["cnT"]])
            if kind == 0:
                ab["maskT"] = Buf(sb("maskT", [128, self.NB, T], BF16), "maskT")
                ab["sc"] = Buf(self.z_full, "sc")
                ab["mk"] = Buf(self.gT_full[:, 0:S], "mk")
                ab["sc1"] = Buf(self.x_full, "sc1")
                ab["sc1"].alias = self.x
                for b_ in self.x:
                    b_.alias = [ab["sc1"]]
                ab["sc"].alias = self.z
                ab["mk"].alias = self.gT[0:(2 * S + T * 2 - 1) // (T * 2)]
                for b in ab["sc"].alias:
                    b.alias = [ab["sc"]]
                for b in ab["mk"].alias:
                    b.alias = [ab["mk"]]
                ab["qib"] = Buf(sb("qib", [64, 8, 128], BF16), "qib")
                ab["kib"] = Buf(sb("kib", [64, S], BF16), "kib")
                ab["wib"] = Buf(sb("wib", [128, 8], F32), "wib")
                ab["bis"] = Buf(sb("bis", [128, 8], F32), "bis")
                ab["bisg"] = Buf(sb("bisg", [128, 2], mybir.dt.uint32), "bisg")
            for j in range(self.NT):
                K.maybe_epoch()
                masked = None
                if kind == 0:
                    self.dsa_topk(j, ab)
                    masked = ab["maskT"]
                self.load_x(self.xs[0], self.xs_buf[0], j, False)
                self.attention(j, dk, scale, masked, ab)
                for m in range(KC):
                    if m % 2 == 0:
                        slot = K.wnext(("wo", L, m // 2), 2048)
                    ps = self.psum[m % 2]
                    off = (m % 2) * 1024
                    self.mm(ps, [(slot.ap[:, off + c * 128: off + (c + 1) * 128], self.OT[c].ap) for c in range(KC)],
                            reads=[slot] + self.OT)
                    K.op(K.dve, lambda: dve.tensor_tensor(self.z[m].ap, ps.ap, self.x[m].ap, ALU.add),
                         reads=[ps, self.x[m]], writes=[self.z[m]])
                self.layernorm(gcol=L * 24 + 8, bcol=96 + L * 24 + 8, scale_after=True)
                self.ffn(L, 1)
                self.layernorm(gcol=L * 24 + 16, bcol=96 + L * 24 + 16, scale_after=False)
                self.store_x(dst, dst_buf, j)
            K.barrier(False)
            if kind == 0:
                for b in self.z + self.gT + self.x:
                    b.alias = []

    def emit(self):
        K, nc = self.K, self.nc
        pool, dve = K.pool.h, K.dve.h
        CH = 128 * 8192
        for i in range(self.wtotal // CH):
            K.dma(K.pool, K.wbf[i * CH:(i + 1) * CH].rearrange("(p n) -> p n", p=128),
                  self.wsrc[i * CH:(i + 1) * CH].rearrange("(p n) -> p n", p=128), reads=[], writes=[K.wbf_bufs[i]])
        K.dma(K.sp, self.prm_sb.ap, self.prm, writes=[self.prm_sb])
        K.dma(K.sp, self.cmb.ap, self.cmb_in, writes=[self.cmb])
        K.dma(K.sp, self.cneg.ap, self.cneg_in, writes=[self.cneg])
        K.op(K.pool, lambda: pool.memset(self.ones32.ap, 1.0), writes=[self.ones32])
        K.op(K.dve, lambda: dve.tensor_scalar(self.prm2_sb.ap, self.prm_sb.ap, ALPHA, None, ALU.mult), reads=[self.prm_sb], writes=[self.prm2_sb])
        for b in (self.ones32, self.cmb, self.cneg, self.prm_sb, self.prm2_sb):
            b.const = True
        for b in K.wbf_bufs:
            b.const = True
        if self.start_barrier:
            K.barrier(False)

        src, src_buf = self.xin, self.xin_buf
        for li, L in enumerate(self.layers):
            last = (li == len(self.layers) - 1)
            if self.stop_after == "a" and last:
                self.phase_a(L, src, src_buf)
                for j in range(self.NT):
                    self.load_x(self.xs[0], self.xs_buf[0], j, False)
                    self.store_x(self.xout, self.xout_buf, j)
                break
            self.phase_a(L, src, src_buf)
            if last:
                self.phase_b(L, self.xout, self.xout_buf, True)
            else:
                self.phase_b(L, self.xs[1], self.xs_buf[1], False)
                src, src_buf = self.xs[1], self.xs_buf[1]
        K.barrier(False)


def _wt(W, cols, nk):
    cols = np.asarray(cols)
    return W[:nk * 128][:, cols].reshape(nk, 128, len(cols)).transpose(1, 0, 2)


def _swap_idx(base, width, rot):
    idx = np.arange(base, base + width)
    h = rot // 2
    idx[:h] = np.arange(base + h, base + rot)
    idx[h:rot] = np.arange(base, base + h)
    return idx


def host_block(key, inp):
    kind = key[0]
    if kind == "w13":
        _, L, which, cp = key
        w = inp["ffn1_w13" if which == 0 else "ffn2_w13"][L]
        out = np.empty((128, 2, 2, KC, 128), np.float32)
        for c2 in range(2):
            c = cp * 2 + c2
            for gu in range(2):
                out[:, c2, gu] = _wt(w, np.arange(gu * DFF + c * 128, gu * DFF + (c + 1) * 128), KC)
        return out.reshape(128, -1)
    if kind == "w2":
        _, L, which, m = key
        w = inp["ffn1_w2" if which == 0 else "ffn2_w2"][L]
        return _wt(w, np.arange(m * 128, (m + 1) * 128), FC).reshape(128, -1)
    if kind == "wo":
        _, L, mp = key
        w = inp["w_out"][L]
        out = np.empty((128, 2, KC, 128), np.float32)
        for m2 in range(2):
            m = mp * 2 + m2
            out[:, m2] = _wt(w, np.arange(m * 128, (m + 1) * 128), KC)
        return out.reshape(128, -1)
    L = key[1]
    mk = L % 3
    if mk == 0:
        w = inp["dsa_w_in"][L // 3]
    elif mk == 1:
        w = inp["fox_w_in"][L // 3]
    else:
        w = inp["mla_w_dqkv"][L // 3]
    if kind == "pj" and mk == 0:
        gs = key[2]
        out = np.zeros((128, 2, 2, KC, 128), np.float32)
        for g2 in range(2):
            g = gs * 2 + g2
            if g > 20:
                continue
            if g < 8:
                bases = [g * 128, g * 128 + 64]
            elif g < 16:
                bases = [1024 + (g - 8) * 128, 1024 + (g - 8) * 128 + 64]
            elif g < 20:
                bases = [3072 + (g - 16) * 128, 3072 + (g - 16) * 128 + 64]
            else:
                bases = [3584, 3584]
            main = np.concatenate([np.arange(b0, b0 + 64) for b0 in bases])
            swap = np.concatenate([_swap_idx(b0, 64, 16) for b0 in bases])
            out[:, g2, 0] = _wt(w, main, KC)
            out[:, g2, 1] = _wt(w, swap, KC)
        return out.reshape(128, -1)
    if kind == "pj" and mk == 1:
        gs = key[2]
        out = np.zeros((128, 4, KC, 128), np.float32)
        for g4 in range(4):
            g = gs * 4 + g4
            base = g * 128 if g < 8 else 1024 + (g - 8) * 128
            out[:, g4] = _wt(w, np.arange(base, base + 128), KC)
        return out.reshape(128, -1)
    if kind == "pj" and mk == 2:
        gs = key[2]
        out = np.zeros((128, 4, KC, 128), np.float32)
        for g4 in range(4):
            g = gs * 4 + g4
            if g < 5:
                out[:, g4] = _wt(w, np.arange(g * 128, (g + 1) * 128), KC)
        return out.reshape(128, -1)
    if kind == "pf":
        return _wt(w, np.arange(3072, 3088), KC).reshape(128, -1)
    if kind == "pkr":
        out = np.zeros((128, 2, KC, 96), np.float32)
        out[:, 0, :, 64:96] = _wt(w, np.arange(640, 672), KC)
        out[:, 1, :, 64:96] = _wt(w, _swap_idx(640, 32, 32), KC)
        return out.reshape(128, -1)
    if kind == "pv":
        half = key[2]
        return _wt(w, np.arange(2048 + half * 512, 2048 + (half + 1) * 512), KC).reshape(128, -1)
    if kind == "pw":
        return _wt(w, np.arange(3648, 3656), KC).reshape(128, -1)
    if kind == "uq":
        hq = key[2]
        wq = inp["mla_w_uq"][L // 3]
        out = np.zeros((128, 4, 2, 3, 96), np.float32)
        for h4 in range(4):
            h = hq * 4 + h4
            main = np.arange(h * 96, (h + 1) * 96)
            swap = np.concatenate([np.arange(h * 96, h * 96 + 64), _swap_idx(h * 96 + 64, 32, 32)])
            out[:, h4, 0] = _wt(wq, main, 3)
            out[:, h4, 1] = _wt(wq, swap, 3)
        return out.reshape(128, -1)
    if kind == "uk":
        wk = inp["mla_w_ukv"][L // 3]
        out = np.zeros((128, 8, 2, 128), np.float32)
        for i in range(8):
            cols = np.concatenate([np.arange(2 * i * 128, 2 * i * 128 + 64), np.arange((2 * i + 1) * 128, (2 * i + 1) * 128 + 64)])
            out[:, i] = _wt(wk, cols, 2)
        return out.reshape(128, -1)
    if kind == "uv":
        wk = inp["mla_w_ukv"][L // 3]
        cols = np.concatenate([np.arange(h * 128 + 64, h * 128 + 128) for h in range(H)])
        return _wt(wk, cols, 2).reshape(128, -1)
    raise KeyError(key)


def host_consts(S):
    import ml_dtypes
    f32 = np.float32
    tabs = np.zeros((4, 128, S), f32)
    pos = np.arange(S, dtype=f32)
    inv = (f32(ROPE_THETA) ** (-np.arange(0, 16, 2, dtype=f32) / f32(16))).astype(f32)
    ang = (pos[None, :] * inv[:, None]).astype(f32)
    c, s_ = np.cos(ang).astype(f32), np.sin(ang).astype(f32)
    tabs[0] = 1.0
    for hb in (0, 64):
        tabs[0, hb:hb + 8] = c
        tabs[0, hb + 8:hb + 16] = c
        tabs[1, hb:hb + 8] = -s_
        tabs[1, hb + 8:hb + 16] = s_
    invm = (f32(ROPE_THETA) ** (-np.arange(0, 32, 2, dtype=f32) / f32(32))).astype(f32)
    angm = (pos[None, :] * invm[:, None]).astype(f32)
    cm_, sm_ = np.cos(angm).astype(f32), np.sin(angm).astype(f32)
    tabs[2] = 1.0
    tabs[2, 64:80] = cm_
    tabs[2, 80:96] = cm_
    tabs[3, 64:80] = -sm_
    tabs[3, 80:96] = sm_
    cmb = np.zeros((128, 4 * 512 + 128), f32)
    p = np.arange(128)[:, None]
    q = np.arange(512)[None, :]
    for a in range(4):
        cmb[:, a * 512:(a + 1) * 512] = ((a * 128 + p) <= q)
    cmb[:, 2048:2176] = np.eye(128, dtype=f32)
    cneg = np.where(np.arange(128)[None, :] <= p, 0.0, NEG).astype(f32)
    cnT = np.where(cmb[:, 0:2048] > 0, 0.0, NEG).astype(f32)
    return tabs, cmb.astype(ml_dtypes.bfloat16), cneg, cnT


_CACHE = {}


def get_prog(S, layers, stop_after=None):
    key = (S, tuple(layers), stop_after)
    if key not in _CACHE:
        pl = Prog(S, layers, True, stop_after=stop_after).plan
        pr = Prog(S, layers, False, plan=pl, stop_after=stop_after)
        _CACHE[key] = (pl, pr)
    return _CACHE[key]


def run(inputs, S=4096, layers=(0, 1, 2, 3), n_cores=8, trace=False, stop_after=None):
    plan, prog = get_prog(S, layers, stop_after)
    inp = {k: np.asarray(v) for k, v in inputs.items()}
    wsrc = np.zeros((plan["wtotal"],), np.float32)
    for key, (off, n) in plan["wblocks"].items():
        wsrc[off:off + 128 * n] = host_block(key, inp).reshape(-1)
    prm = np.zeros((128, 256), np.float32)
    prm[:, 0:96] = inp["ln_g"].reshape(DEPTH * 3 * KC, 128).T
    prm[:, 96:192] = inp["ln_b"].reshape(DEPTH * 3 * KC, 128).T
    prm[0:16, 192] = inp["fox_b_f"][0]
    prm[:, 193:196] = inp["mla_q_norm_g"][0].reshape(3, 128).T
    prm[:, 196:198] = inp["mla_kv_norm_g"][0].reshape(2, 128).T
    tabs, cmb, cneg, cnT = host_consts(S)
    x = inp["x"]
    in_maps = []
    for c in range(n_cores):
        in_maps.append({"xT": np.ascontiguousarray(x[c, :S].T), "wsrc": wsrc, "prm": prm, "tabs": tabs, "cmb_in": cmb, "cneg_in": cneg, "cnT_in": cnT})
    res = run_bass_kernel_spmd(prog.nc, in_maps, core_ids=list(range(n_cores)), trace=trace)
    out = np.stack([res.results[c]["outT"].T for c in range(n_cores)], axis=0)
    return out, res


def kernel(**inputs):
    out, _ = run(inputs)
    return np.ascontiguousarray(out.astype(np.float32))
```

```python
import numpy as np
import concourse.bass as bass
import concourse.mybir as mybir
from concourse.bass_utils import run_bass_kernel_spmd

F32 = mybir.dt.float32
BF16 = mybir.dt.bfloat16
AF = mybir.ActivationFunctionType
ALU = mybir.AluOpType
AX = mybir.AxisListType

D = 1024
KC = 8
T = 512
DFF = 2816
FC = 22
H = 16
DEPTH = 4
ALPHA = (2.0 * DEPTH) ** 0.25
LN_EPS = 1e-5
RMS_EPS = 1e-6
TOPK = 256
SLOT = 4096
NSLOT = 4
ROPE_THETA = 500000.0
NEG = -1.0e30


class Sem:
    def __init__(self, h, sid):
        self.h = h
        self.id = sid
        self.dead = False


class Eng:
    def __init__(self, name, h, compute):
        self.name = name
        self.h = h
        self.compute = compute
        self.sem = None
        self.count = 0
        self.waited = {}


class Buf:
    def __init__(self, ap, name="", const=False, untracked=False):
        self.ap = ap
        self.name = name
        self.const = const
        self.untracked = untracked
        self.w = None
        self.r = {}
        self.alias = []


def _exp(bufs):
    out = []
    for b in bufs:
        out.append(b)
        out.extend(b.alias)
    return out


class KB:
    def __init__(self, nc, planning):
        self.nc = nc
        self.planning = planning
        self.nsem = 0
        self.pe = Eng("pe", nc.tensor, True)
        self.act = Eng("act", nc.scalar, True)
        self.dve = Eng("dve", nc.vector, True)
        self.pool = Eng("pool", nc.gpsimd, True)
        self.sp = Eng("sp", nc.sync, False)
        self.engines = [self.pe, self.act, self.dve, self.pool, self.sp]
        self.dma_pool = []
        self.dma_rr = 0
        self.wplan = []
        self.wblocks = {}
        self.woff = 0
        self.wuse = 0
        self.wloaded = 0
        self.wseq = None
        self.ninstr = 0
        if not planning:
            for e in self.engines:
                if e.compute:
                    e.sem = self.new_sem(e.name)
            self.dma_pools = {}
            for q in (self.sp, self.pool, self.act):
                self.dma_pools[q.name] = [[self.new_sem("dma%s%d" % (q.name, i)), 0] for i in range({"sp": 12, "pool": 4, "act": 12}[q.name])]
            self.dma_rrs = {"sp": 0, "pool": 0, "act": 0}

    def new_sem(self, name):
        self.nsem += 1
        h = self.nc.alloc_semaphore(name="%s_%d" % (name, self.nsem))
        return Sem(h, self.nsem)

    def _wait(self, eng, tok):
        sem, val = tok
        if sem.dead:
            return
        if eng.waited.get(sem.id, 0) >= val:
            return
        eng.h.wait_ge(sem.h, val)
        eng.waited[sem.id] = val
        self.ninstr += 1

    def _deps(self, eng, reads, writes):
        reads = [b for b in _exp(reads) if not b.untracked]
        writes = [b for b in _exp(writes) if not b.untracked]
        skip_same = eng is self.pe
        for b in reads:
            if b.w is not None and not (skip_same and b.w[0] is eng.sem):
                self._wait(eng, b.w)
        for b in writes:
            if b.w is not None and not (skip_same and b.w[0] is eng.sem):
                self._wait(eng, b.w)
            for tok in b.r.values():
                if not (skip_same and tok[0] is eng.sem):
                    self._wait(eng, tok)

    def _mark(self, tok, reads, writes):
        reads = [b for b in reads if not b.untracked]
        writes = [b for b in writes if not b.untracked]
        for b in reads:
            if not b.const:
                b.r[tok[0].id] = tok
        for b in writes:
            b.w = tok
            b.r = {}

    def op(self, eng, fn, reads=(), writes=()):
        if self.planning:
            return
        self._deps(eng, reads, writes)
        ins = fn()
        eng.count += 1
        ins.then_inc(eng.sem.h, 1)
        self.ninstr += 1
        self._mark((eng.sem, eng.count), reads, writes)

    def dma(self, q, out_ap, in_ap, reads=(), writes=()):
        if self.planning:
            return
        self._deps(q, reads, writes)
        pl = self.dma_pools[q.name]
        ent = pl[self.dma_rrs[q.name]]
        self.dma_rrs[q.name] = (self.dma_rrs[q.name] + 1) % len(pl)
        if ent[1] > 0:
            self._wait(q, (ent[0], ent[1]))
        ent[1] += 16
        q.h.dma_start(out=out_ap, in_=in_ap).then_inc(ent[0].h, 16)
        self.ninstr += 1
        self._mark((ent[0], ent[1]), reads, writes)

    def barrier(self, new_epoch=True):
        if self.planning:
            return
        toks = [(e.sem, e.count) for e in self.engines if e.compute and e.count > 0]
        toks += [(ent[0], ent[1]) for pl in self.dma_pools.values() for ent in pl if ent[1] > 0]
        for e in self.engines:
            for tok in toks:
                if tok[0] is not e.sem:
                    self._wait(e, tok)
        if new_epoch:
            for e in self.engines:
                if e.compute:
                    e.sem.dead = True
                    e.sem = self.new_sem(e.name)
                    e.count = 0

    def maybe_epoch(self):
        if self.planning:
            return
        if max(e.count for e in self.engines if e.compute) > 9000:
            self.barrier(True)

    def wnext(self, key, n, hold=0):
        if self.planning:
            if key not in self.wblocks:
                self.wblocks[key] = (self.woff, n)
                self.woff += 128 * n
            self.wplan.append(key)
            return self.slots[0]
        u = self.wuse
        assert self.wseq[u] == key, (self.wseq[u], key)
        while self.wloaded < len(self.wseq) and self.wloaded < u + NSLOT - hold:
            k2 = self.wseq[self.wloaded]
            off, n2 = self.wblocks[k2]
            sl = self.slots[self.wloaded % NSLOT]
            src = self.wbf[off:off + 128 * n2].rearrange("(p n) -> p n", p=128)
            CH = 128 * 8192
            rb = self.wbf_bufs[off // CH:(off + 128 * n2 - 1) // CH + 1]
            self.dma(self.sp, sl.ap[:, 0:n2], src, reads=rb, writes=[sl])
            self.wloaded += 1
        self.wuse += 1
        return self.slots[u % NSLOT]


def _arr(ap3, name, n):
    return [Buf(ap3[:, i, :], "%s%d" % (name, i)) for i in range(n)]


class Prog:
    def __init__(self, S, layers, planning, plan=None, stop_after=None):
        self.S = S
        self.NT = S // T
        self.NB = S // 128
        self.layers = layers
        self.stop_after = stop_after
        import os
        self.start_barrier = os.environ.get("KB_START_BARRIER", "1") == "1"
        nc = bass.Bass("TRN2", target_bir_lowering=False)
        self.nc = nc
        K = KB(nc, planning)
        self.K = K
        if plan is not None:
            K.wseq = plan["wseq"]
            K.wblocks = plan["wblocks"]
            self.wtotal = plan["wtotal"]
        else:
            self.wtotal = 128 * SLOT
        self.alloc()
        self.emit()
        if planning:
            tot = K.woff
            pad = (-tot) % (128 * 8192)
            self.plan = {"wseq": K.wplan, "wblocks": K.wblocks, "wtotal": tot + pad}

    def alloc(self):
        nc, K, S = self.nc, self.K, self.S
        dt = nc.dram_tensor
        self.xin = dt("xT", [D, S], F32, kind="ExternalInput").ap()
        self.wsrc = dt("wsrc", [self.wtotal], F32, kind="ExternalInput").ap()
        self.prm = dt("prm", [128, 256], F32, kind="ExternalInput").ap()
        self.tabs = dt("tabs", [4, 128, S], F32, kind="ExternalInput").ap()
        self.cmb_in = dt("cmb_in", [128, 4 * 512 + 128], BF16, kind="ExternalInput").ap()
        self.cneg_in = dt("cneg_in", [128, 128], F32, kind="ExternalInput").ap()
        self.cnT_in = dt("cnT_in", [128, 4 * T], F32, kind="ExternalInput").ap()
        self.xout = dt("outT", [D, S], F32, kind="ExternalOutput").ap()
        K.wbf = dt("wbf", [self.wtotal], BF16).ap()
        K.wbf_bufs = [Buf(K.wbf, "wbf%d" % i) for i in range(self.wtotal // (128 * 8192))]
        self.xs = [dt("xs%d" % i, [D, S], F32).ap() for i in range(2)]
        self.xs_buf = [Buf(self.xs[i], "xs%d" % i, untracked=True) for i in range(2)]
        self.QT = dt("QT", [H, 128, S], BF16).ap()
        self.KT = dt("KT", [H, 128, S], BF16).ap()
        self.VS = dt("VS", [H, 128, self.NB, 64], BF16).ap()
        self.QI = dt("QI", [8, 64, S], BF16).ap()
        self.KI = dt("KI", [64, S], BF16).ap()
        self.WI = dt("WI", [S, 8], F32).ap()
        self.QT_buf, self.KT_buf, self.VS_buf = Buf(self.QT, "QT", untracked=True), Buf(self.KT, "KT", untracked=True), Buf(self.VS, "VS", untracked=True)
        self.QI_buf, self.KI_buf, self.WI_buf = Buf(self.QI, "QI", untracked=True), Buf(self.KI, "KI", untracked=True), Buf(self.WI, "WI", untracked=True)
        self.xin_buf = Buf(self.xin, "xin", const=True)
        self.xout_buf = Buf(self.xout, "xout", untracked=True)

        sb = lambda name, shape, dtype: nc.alloc_sbuf_tensor(name, shape, dtype).ap()
        ring = sb("ring", [128, NSLOT, SLOT], BF16)
        K.slots = [Buf(ring[:, i, :], "slot%d" % i) for i in range(NSLOT)]
        xt_ = sb("x", [128, KC, T], F32)
        self.x = _arr(xt_, "x", KC)
        self.x_full = xt_.rearrange("p a t -> p (a t)")
        self.xb = _arr(sb("xb", [128, KC, T], BF16), "xb", KC)
        zt = sb("z", [128, KC, T], F32)
        self.z = _arr(zt, "z", KC)
        self.z_full = zt.rearrange("p a t -> p (a t)")
        gt = sb("gT", [128, FC, T], BF16)
        self.gT = _arr(gt, "gT", FC)
        self.gT_full = gt.rearrange("p a t -> p (a t)")
        self.tmp = _arr(sb("tmp", [128, 4, T], F32), "tmp", 4)
        self.stat = _arr(sb("stat", [128, 4, T], F32), "stat", 4)
        self.prm_sb = Buf(sb("prm_sb", [128, 256], F32), "prm_sb")
        self.prm2_sb = Buf(sb("prm2_sb", [128, 256], F32), "prm2_sb")
        self.ones32 = Buf(sb("ones32", [128, 128], F32), "ones32")
        self.cmb = Buf(sb("cmb", [128, 4 * 512 + 128], BF16), "cmb")
        self.cneg = Buf(sb("cneg", [128, 128], F32), "cneg")
        self.psum = [Buf(nc.alloc_psum_tensor("ps%d" % i, [128, 512], F32).ap(), "ps%d" % i) for i in range(8)]

    def mm(self, ps, pairs, reads, M=128, N=T, pbase=0):
        K = self.K
        pe = K.pe.h
        out = ps.ap[pbase:pbase + M, 0:N]

        def f():
            ins = None
            n = len(pairs)
            for i, (l, r) in enumerate(pairs):
                ins = pe.matmul(out, l, r, start=(i == 0), stop=(i == n - 1))
            K.ninstr += n - 1
            return ins
        K.op(K.pe, f, reads=reads, writes=[ps])

    def prmcol(self, c, n=128):
        return self.prm_sb.ap[0:n, c:c + 1]

    def ffn(self, L, which, side=None):
        K = self.K
        act, dve = K.act.h, K.dve.h
        xb, gT, z, x = self.xb, self.gT, self.z, self.x
        for cp in range(FC // 2):
            slot = K.wnext(("w13", L, which, cp), 4096)
            for c2 in range(2):
                c = cp * 2 + c2
                psg, psu = self.psum[(c % 2) * 2], self.psum[(c % 2) * 2 + 1]
                for gu, ps in ((0, psg), (1, psu)):
                    base = (c2 * 2 + gu) * KC
                    self.mm(ps, [(slot.ap[:, (base + k) * 128:(base + k + 1) * 128], xb[k].ap) for k in range(KC)],
                            reads=[slot] + xb)
                sg = self.tmp[c % 2]
                K.op(K.act, lambda: act.activation(sg.ap, psg.ap, AF.Silu), reads=[psg], writes=[sg])
                K.op(K.dve, lambda: dve.tensor_tensor(gT[c].ap, psu.ap, sg.ap, ALU.mult), reads=[psu, sg], writes=[gT[c]])
                if side:
                    side.pop(0)()
        for m in range(KC):
            slot = K.wnext(("w2", L, which, m), FC * 128)
            ps = self.psum[4 + m % 2]
            self.mm(ps, [(slot.ap[:, c * 128:(c + 1) * 128], gT[c].ap) for c in range(FC)], reads=[slot] + gT)
            K.op(K.dve, lambda: dve.scalar_tensor_tensor(z[m].ap, ps.ap, 0.5, x[m].ap, ALU.mult, ALU.add),
                 reads=[ps, x[m]], writes=[z[m]])

    def layernorm(self, gcol, bcol, scale_after, defer=False, tbufs=None, after=None):
        K = self.K
        act, dve, pool, pe = K.act.h, K.dve.h, K.pool.h, K.pe.h
        z, x, xb = self.z, self.x, self.xb
        ps_s, ps_q = self.psum[4], self.psum[5]
        mean, var, std, rstd = self.stat
        self.mm(ps_s, [(self.ones32.ap, z[m].ap) for m in range(KC)], reads=z + [self.ones32])
        for m in range(KC):
            sq = self.tmp[2 + m % 2]
            K.op(K.act, lambda: act.activation(sq.ap, z[m].ap, AF.Square), reads=[z[m]], writes=[sq])
            K.op(K.pe, lambda: pe.matmul(ps_q.ap, self.ones32.ap, sq.ap, start=(m == 0), stop=(m == KC - 1)),
                 reads=[sq, self.ones32], writes=[ps_q])
        K.op(K.dve, lambda: dve.tensor_scalar(mean.ap, ps_s.ap, 1.0 / D, None, ALU.mult), reads=[ps_s], writes=[mean])
        K.op(K.pool, lambda: pool.tensor_tensor(var.ap, mean.ap, mean.ap, ALU.mult), reads=[mean], writes=[var])
        K.op(K.dve, lambda: dve.scalar_tensor_tensor(var.ap, ps_q.ap, 1.0 / D, var.ap, ALU.mult, ALU.subtract),
             reads=[ps_q, var], writes=[var])
        K.op(K.dve, lambda: dve.tensor_scalar(var.ap, var.ap, LN_EPS, None, ALU.add), reads=[var], writes=[var])
        K.op(K.act, lambda: act.activation(std.ap, var.ap, AF.Sqrt), reads=[var], writes=[std])
        K.op(K.dve, lambda: dve.reciprocal(rstd.ap, std.ap), reads=[std], writes=[rstd])
        K.op(K.dve, lambda: dve.scalar_tensor_tensor(var.ap, mean.ap, -1.0, rstd.ap, ALU.mult, ALU.mult), reads=[mean, rstd], writes=[var])
        pb = self.prm2_sb if scale_after else self.prm_sb
        tbuf = tbufs if tbufs is not None else self.tmp
        jobs = []
        for m in range(KC):
            def job(m=m, t=tbuf[m % 2], zm=z[m], xm=x[m], xbm=xb[m]):
                K.op(K.dve, lambda: dve.tensor_tensor(t.ap, zm.ap, rstd.ap, ALU.mult), reads=[zm, rstd], writes=[t])
                K.op(K.pool, lambda: pool.tensor_tensor(t.ap, t.ap, var.ap, ALU.add), reads=[t, var], writes=[t])
                K.op(K.act, lambda: act.activation(xm.ap, t.ap, AF.Identity, bias=pb.ap[:, bcol + m:bcol + m + 1], scale=pb.ap[:, gcol + m:gcol + m + 1]),
                     reads=[t, pb], writes=[xm])
                K.op(K.dve, lambda: dve.tensor_scalar(xbm.ap, t.ap, self.prmcol(gcol + m), self.prmcol(bcol + m), ALU.mult, ALU.add),
                     reads=[t, self.prm_sb], writes=[xbm])
                if after is not None:
                    after(m, xm)
            jobs.append(job)
        if defer:
            return jobs
        for jb in jobs:
            jb()
        return []

    def load_x(self, src, src_buf, j, want_xb, q=None, do_dma=True, do_prep=True):
        K = self.K
        pool = K.pool.h
        q = q or K.sp
        if do_dma:
            for m in range(KC):
                K.dma(q, self.x[m].ap, src[m * 128:(m + 1) * 128, j * T:(j + 1) * T], reads=[src_buf], writes=[self.x[m]])
        if do_prep:
            for m in range(KC):
                if want_xb:
                    K.op(K.pool, lambda: pool.tensor_copy(self.xb[m].ap, self.x[m].ap), reads=[self.x[m]], writes=[self.xb[m]])
                K.op(K.pool, lambda: pool.tensor_scalar(self.x[m].ap, self.x[m].ap, ALPHA, 0.0, ALU.mult, ALU.add),
                     reads=[self.x[m]], writes=[self.x[m]])

    def store_x(self, dst, dst_buf, j):
        K = self.K
        for m in range(KC):
            K.dma(K.act, dst[m * 128:(m + 1) * 128, j * T:(j + 1) * T], self.x[m].ap, reads=[self.x[m]], writes=[dst_buf])

    def proj_group(self, slot, off_main, off_swap, rhs, nk, M, ps_main, ps_swap):
        self.mm(ps_main, [(slot.ap[:, off_main + k * M: off_main + (k + 1) * M], rhs[k].ap) for k in range(nk)],
                reads=[slot] + rhs[:nk], M=M)
        if off_swap is not None:
            self.mm(ps_swap, [(slot.ap[:, off_swap + k * M: off_swap + (k + 1) * M], rhs[k].ap) for k in range(nk)],
                    reads=[slot] + rhs[:nk], M=M)

    def rope_evac(self, ps_main, ps_swap, M, tC, tS, dst, par=0):
        K = self.K
        dve, pool = K.dve.h, K.pool.h
        t1, t2 = self.tmp[(par % 2) * 2], self.tmp[(par % 2) * 2 + 1]
        K.op(K.dve, lambda: dve.tensor_tensor(t1.ap[0:M], ps_main.ap[0:M], tC.ap[0:M], ALU.mult), reads=[ps_main, tC], writes=[t1])
        K.op(K.dve, lambda: dve.tensor_tensor(t2.ap[0:M], ps_swap.ap[0:M], tS.ap[0:M], ALU.mult), reads=[ps_swap, tS], writes=[t2])
        K.op(K.pool, lambda: pool.tensor_tensor(dst.ap[0:M], t1.ap[0:M], t2.ap[0:M], ALU.add), reads=[t1, t2], writes=[dst])

    def vproj(self, L, j, key, src, nk, bufs):
        K = self.K
        act = K.act.h
        if nk * 1024 > SLOT:
            s0 = K.wnext((key, L, 0), nk * 512)
            s1 = K.wnext((key, L, 1), nk * 512, hold=1)
            wap = lambda k, half: (s0, s1)[half].ap[:, k * 512:(k + 1) * 512]
            sl = [s0, s1]
        else:
            s0 = K.wnext((key, L), nk * 1024)
            wap = lambda k, half: s0.ap[:, k * 1024 + half * 512: k * 1024 + (half + 1) * 512]
            sl = [s0]
        for tb in range(4):
            vst = bufs["vst"][tb % 2]
            for half in range(2):
                ps = self.psum[half]
                self.mm(ps, [(src[k].ap[:, tb * 128:(tb + 1) * 128], wap(k, half)) for k in range(nk)], reads=sl + src[:nk])
                K.op(K.act, lambda: act.copy(vst.ap[:, half * 512:(half + 1) * 512], ps.ap), reads=[ps], writes=[vst])
            nb = 4 * j + tb
            K.dma(K.act, self.VS[:, :, nb, :].rearrange("h p d -> p h d"), vst.ap.rearrange("p (h d) -> p h d", d=64),
                  reads=[vst], writes=[self.VS_buf])

    def store_heads(self, dst, dst_buf, stage, i, j, rows=64):
        K = self.K
        K.dma(K.act, dst[2 * i, 0:rows, j * T:(j + 1) * T], stage.ap[0:rows], reads=[stage], writes=[dst_buf])
        K.dma(K.act, dst[2 * i + 1, 0:rows, j * T:(j + 1) * T], stage.ap[64:64 + rows], reads=[stage], writes=[dst_buf])

    def phase_a(self, L, src, src_buf):
        from contextlib import ExitStack
        K, nc = self.K, self.nc
        act, dve, pool = K.act.h, K.dve.h, K.pool.h
        kind = L % 3
        with ExitStack() as es:
            sb = lambda name, shape, dtype: es.enter_context(nc.sbuf_tensor(name + "_a%d" % L, shape, dtype)).ap()
            bufs = {}
            rt = _arr(sb("rt", [128, 2, T], F32), "rt", 2)
            stage = _arr(sb("stage", [128, 2, T], BF16), "stage", 2)
            bufs["vst"] = _arr(sb("vst", [128, 2, 1024], BF16), "vst", 2)
            if kind == 0:
                wst = _arr(sb("wst", [128, 2, 8], F32), "wst", 2)
            if kind == 1:
                cum = _arr(sb("cum", [16, 4, T], F32), "cum", 4)
                cons = Buf(sb("cons", [16, T], F32), "cons")
                cst = Buf(sb("cst", [16, 6, T], BF16), "cst")
                cst1 = Buf(sb("cst1", [16, 3, T], BF16), "cst1")
                carry = Buf(sb("carry", [16, 2], F32), "carry")
                negb = Buf(sb("negb", [16, 1], F32), "negb")
                K.op(K.pool, lambda: pool.memset(cons.ap, 1.0), writes=[cons])
                K.op(K.pool, lambda: pool.memset(cst1.ap, 1.0), writes=[cst1])
                K.op(K.pool, lambda: pool.memset(carry.ap, 0.0), writes=[carry])
                K.op(K.dve, lambda: dve.tensor_scalar(negb.ap, self.prm_sb.ap[0:16, 192:193], -1.0, None, ALU.mult),
                     reads=[self.prm_sb], writes=[negb])
            if kind == 2:
                lat = _arr(sb("lat", [128, 5, T], F32), "lat", 5)
                latn = _arr(sb("latn", [128, 5, T], BF16), "latn", 5)

            x2t = sb("x2", [128, KC, T], F32)
            xb2t = sb("xb2", [128, KC, T], BF16)
            z2t = sb("z2", [128, KC, T], F32)
            sets = [(self.x, self.xb, self.z), (_arr(x2t, "x2_", KC), _arr(xb2t, "xb2_", KC), _arr(z2t, "z2_", KC))]

            def set_par(p):
                self.x, self.xb, self.z = sets[p % 2]

            def proj(j):
                cols = slice(j * T, (j + 1) * T)
                xb = self.xb
                if kind in (0, 2):
                    tb0 = 0 if kind == 0 else 2
                    for t in range(2):
                        K.dma(K.sp, rt[t].ap, self.tabs[tb0 + t, :, cols], reads=[], writes=[rt[t]])
                if kind == 0:
                    for g in range(21):
                        if g % 2 == 0:
                            slot = K.wnext(("pj", L, g // 2), 4096)
                        off = (g % 2) * 2048
                        pm, psw = self.psum[(g % 3) * 2], self.psum[(g % 3) * 2 + 1]
                        self.proj_group(slot, off, off + 1024, xb, KC, 128, pm, psw)
                        st = stage[g % 2]
                        self.rope_evac(pm, psw, 128, rt[0], rt[1], st, par=g)
                        if g < 8:
                            self.store_heads(self.QT, self.QT_buf, st, g, j)
                        elif g < 16:
                            self.store_heads(self.KT, self.KT_buf, st, g - 8, j)
                        elif g < 20:
                            self.store_heads(self.QI, self.QI_buf, st, g - 16, j)
                        else:
                            K.dma(K.act, self.KI[:, cols], st.ap[0:64], reads=[st], writes=[self.KI_buf])
                    self.vproj(L, j, "pv", xb, KC, bufs)
                    slot = K.wnext(("pw", L), 64)
                    for tb in range(4):
                        ps = self.psum[2]
                        self.mm(ps, [(xb[k].ap[:, tb * 128:(tb + 1) * 128], slot.ap[:, k * 8:(k + 1) * 8]) for k in range(KC)],
                                reads=[slot] + xb, N=8)
                        w = wst[tb % 2]
                        K.op(K.dve, lambda: dve.tensor_copy(w.ap, ps.ap[:, 0:8]), reads=[ps], writes=[w])
                        K.dma(K.act, self.WI[j * T + tb * 128: j * T + (tb + 1) * 128, :], w.ap, reads=[w], writes=[self.WI_buf])
                elif kind == 1:
                    for g in range(16):
                        if g % 4 == 0:
                            slot = K.wnext(("pj", L, g // 4), 4096)
                        pm = self.psum[g % 4]
                        self.proj_group(slot, (g % 4) * 1024, None, xb, KC, 128, pm, None)
                        st = stage[g % 2]
                        K.op(K.act, lambda: act.copy(st.ap, pm.ap), reads=[pm], writes=[st])
                        if g < 8:
                            self.store_heads(self.QT, self.QT_buf, st, g, j)
                        else:
                            self.store_heads(self.KT, self.KT_buf, st, g - 8, j)
                    slot = K.wnext(("pf", L), 128)
                    pm = self.psum[0]
                    self.proj_group(slot, 0, None, xb, KC, 16, pm, None)
                    e, l1, r1, r2 = cum
                    K.op(K.act, lambda: act.activation(e.ap, pm.ap[0:16], AF.Exp, bias=negb.ap, scale=-1.0), reads=[pm, negb], writes=[e])
                    K.op(K.act, lambda: act.activation(l1.ap, e.ap, AF.Ln, bias=1.0, scale=1.0), reads=[e], writes=[l1])
                    K.op(K.dve, lambda: dve.tensor_tensor_scan(e.ap, cons.ap, l1.ap, carry.ap[:, 0:1], ALU.mult, ALU.add),
                         reads=[cons, l1, carry], writes=[e])
                    K.op(K.dve, lambda: dve.tensor_copy(carry.ap[:, 0:1], e.ap[:, T - 1:T]), reads=[e], writes=[carry])
                    K.op(K.dve, lambda: dve.tensor_scalar(l1.ap, e.ap, 8.0, None, ALU.mult), reads=[e], writes=[l1])
                    K.op(K.dve, lambda: dve.tensor_copy(cst.ap[:, 3, :], l1.ap), reads=[l1], writes=[cst])
                    K.op(K.dve, lambda: dve.tensor_tensor(r1.ap, l1.ap, cst.ap[:, 3, :], ALU.subtract), reads=[l1, cst], writes=[r1])
                    K.op(K.dve, lambda: dve.tensor_copy(cst.ap[:, 4, :], r1.ap), reads=[r1], writes=[cst])
                    K.op(K.dve, lambda: dve.tensor_tensor(r2.ap, r1.ap, cst.ap[:, 4, :], ALU.subtract), reads=[r1, cst], writes=[r2])
                    K.op(K.dve, lambda: dve.tensor_copy(cst.ap[:, 5, :], r2.ap), reads=[r2], writes=[cst])
                    K.op(K.dve, lambda: dve.tensor_scalar(cst.ap[:, 0:3, :], cst.ap[:, 3:6, :], -1.0, None, ALU.mult), reads=[cst], writes=[cst])
                    K.dma(K.act, self.QT[:, 64:67, cols], cst.ap[:, 0:3, :], reads=[cst], writes=[self.QT_buf])
                    K.dma(K.act, self.KT[:, 67:70, cols], cst.ap[:, 3:6, :], reads=[cst], writes=[self.KT_buf])
                    K.dma(K.act, self.QT[:, 67:70, cols], cst1.ap, reads=[cst1], writes=[self.QT_buf])
                    K.dma(K.act, self.KT[:, 64:67, cols], cst1.ap, reads=[cst1], writes=[self.KT_buf])
                    self.vproj(L, j, "pv", xb, KC, bufs)
                else:
                    for g in range(5):
                        if g % 4 == 0:
                            slot = K.wnext(("pj", L, g // 4), 4096)
                        pm = self.psum[g % 4]
                        self.proj_group(slot, (g % 4) * 1024, None, xb, KC, 128, pm, None)
                        K.op(K.act, lambda: act.copy(lat[g].ap, pm.ap), reads=[pm], writes=[lat[g]])
                    slot = K.wnext(("pkr", L), 2 * KC * 96)
                    pm, psw = self.psum[0], self.psum[1]
                    self.proj_group(slot, 0, KC * 96, xb, KC, 96, pm, psw)
                    st = stage[0]
                    self.rope_evac(pm, psw, 96, rt[0], rt[1], st)
                    for h in range(H):
                        K.dma(K.act, self.KT[h, 64:96, cols], st.ap[64:96], reads=[st], writes=[self.KT_buf])
                    for (m0, nm, gc) in ((0, 3, 193), (3, 2, 196)):
                        ps_q = self.psum[4]
                        for mi in range(nm):
                            sq = self.tmp[2 + mi % 2]
                            K.op(K.act, lambda: act.activation(sq.ap, lat[m0 + mi].ap, AF.Square), reads=[lat[m0 + mi]], writes=[sq])
                            K.op(K.pe, lambda: K.pe.h.matmul(ps_q.ap, self.ones32.ap, sq.ap, start=(mi == 0), stop=(mi == nm - 1)),
                                 reads=[sq, self.ones32], writes=[ps_q])
                        var, std, rstd = self.stat[1], self.stat[2], self.stat[3]
                        K.op(K.dve, lambda: dve.tensor_scalar(var.ap, ps_q.ap, 1.0 / (nm * 128), RMS_EPS, ALU.mult, ALU.add),
                             reads=[ps_q], writes=[var])
                        K.op(K.act, lambda: act.activation(std.ap, var.ap, AF.Sqrt), reads=[var], writes=[std])
                        K.op(K.dve, lambda: dve.reciprocal(rstd.ap, std.ap), reads=[std], writes=[rstd])
                        for mi in range(nm):
                            K.op(K.dve, lambda: dve.scalar_tensor_tensor(latn[m0 + mi].ap, lat[m0 + mi].ap, self.prmcol(gc + mi), rstd.ap,
                                                                         ALU.mult, ALU.mult),
                                 reads=[lat[m0 + mi], rstd, self.prm_sb], writes=[latn[m0 + mi]])
                    for h in range(H):
                        if h % 4 == 0:
                            slot = K.wnext(("uq", L, h // 4), 4 * 576)
                        pm, psw = self.psum[(h % 2) * 2], self.psum[(h % 2) * 2 + 1]
                        off = (h % 4) * 576
                        self.proj_group(slot, off, off + 288, latn[0:3], 3, 96, pm, psw)
                        st = stage[h % 2]
                        self.rope_evac(pm, psw, 96, rt[0], rt[1], st, par=h)
                        K.dma(K.act, self.QT[h, 0:96, cols], st.ap[0:96], reads=[st], writes=[self.QT_buf])
                    slot = K.wnext(("uk", L), 8 * 256)
                    for i in range(8):
                        pm = self.psum[i % 4]
                        self.proj_group(slot, i * 256, None, latn[3:5], 2, 128, pm, None)
                        st = stage[i % 2]
                        K.op(K.act, lambda: act.copy(st.ap, pm.ap), reads=[pm], writes=[st])
                        self.store_heads(self.KT, self.KT_buf, st, i, j)
                    self.vproj(L, j, "uv", latn[3:5], 2, bufs)

            def ld(j):
                set_par(j)
                self.load_x(src, src_buf, j, True, q=K.act, do_prep=False)

            def prep(j):
                set_par(j)
                self.load_x(src, src_buf, j, True, do_dma=False)

            lnt = _arr(sb("lnt", [128, 2, T], F32), "lnt", 2)
            pending = []

            def ffn_front(j):
                set_par(j)
                self.ffn(L, 0, side=pending)
                while pending:
                    pending.pop(0)()

            def ffn_ln_back(j):
                set_par(j)

                def st(m, xm):
                    K.dma(K.act, self.xs[0][m * 128:(m + 1) * 128, j * T:(j + 1) * T], xm.ap, reads=[xm], writes=[self.xs_buf[0]])
                jobs = self.layernorm(gcol=L * 24 + 0, bcol=96 + L * 24 + 0, scale_after=False, defer=True, tbufs=lnt, after=st)
                pending.extend(jobs)

            def flush():
                while pending:
                    pending.pop(0)()

            NT = self.NT
            ld(0)
            prep(0)
            ffn_front(0)
            ffn_ln_back(0)
            flush()
            if NT > 1:
                ld(1)
                prep(1)
            for j in range(NT):
                K.maybe_epoch()
                if j + 1 < NT:
                    ffn_front(j + 1)
                if j + 2 < NT:
                    ld(j + 2)
                flush()
                set_par(j)
                proj(j)
                if j + 2 < NT:
                    prep(j + 2)
                if j + 1 < NT:
                    ffn_ln_back(j + 1)
                    if j + 2 >= NT:
                        flush()
            set_par(0)
            K.barrier(False)

    def attention(self, j, dk, scale, masked, ab):
        K = self.K
        act, dve, pool, pe = K.act.h, K.dve.h, K.pool.h, K.pe.h
        nk = (j + 1) * T
        nb = 4 * (j + 1)
        cols = slice(j * T, (j + 1) * T)
        ktb, vev, vod, qtb, pT, rec, bcs = ab["ktb"], ab["vev"], ab["vod"], ab["qtb"], ab["pT"], ab["rec"], ab["bcs"]
        pend = None
        for h in range(H):
            odd = h % 2
            kt = ktb[h % 2]
            qt = qtb[h % 2]
            vb = vod if odd else vev
            K.dma(K.sp, kt.ap[0:dk, 0:nk], self.KT[h, 0:dk, 0:nk], reads=[self.KT_buf], writes=[kt])
            K.dma(K.sp, qt.ap[0:dk, :], self.QT[h, 0:dk, cols], reads=[self.QT_buf], writes=[qt])
            vdst = vb.ap[:, 0:nb, 64:128] if odd else vb.ap[:, 0:nb, 0:64]
            K.dma(K.sp, vdst, self.VS[h, :, 0:nb, :], reads=[self.VS_buf], writes=[vb])
            ps_o = self.psum[4 + odd]
            SB = 4

            def smm(kb, kt=kt, qt=qt):
                self.mm(self.psum[kb % SB], [(kt.ap[0:dk, kb * 128:(kb + 1) * 128], qt.ap[0:dk, :])], reads=[kt, qt])
            for kb in range(min(SB, nb)):
                smm(kb)
            if pend is not None:
                pend()
            for kb in range(nb):
                ps_s = self.psum[kb % SB]
                p = pT[kb % 4]
                if masked is None and kb >= 4 * j:
                    a = kb - 4 * j
                    tt = self.tmp[kb % 2]
                    K.op(K.dve, lambda: dve.tensor_tensor(tt.ap, ps_s.ap, ab["cnT"].ap[:, a * 512:(a + 1) * 512], ALU.add),
                         reads=[ps_s, ab["cnT"]], writes=[tt])
                    K.op(K.act, lambda: act.activation(p.ap, tt.ap, AF.Exp, scale=scale), reads=[tt], writes=[p])
                else:
                    K.op(K.act, lambda: act.activation(p.ap, ps_s.ap, AF.Exp, scale=scale), reads=[ps_s], writes=[p])
                if masked is not None:
                    mk = masked.ap[:, kb, :]
                    if kb % 2 == 0:
                        K.op(K.dve, lambda: dve.tensor_tensor(p.ap, p.ap, mk, ALU.mult), reads=[p, masked], writes=[p])
                    else:
                        K.op(K.pool, lambda: pool.tensor_tensor(p.ap, p.ap, mk, ALU.mult), reads=[p, masked], writes=[p])
                if odd:
                    lhsT, M = vb.ap[:, kb, 0:128], 128
                else:
                    lhsT, M = vb.ap[:, kb, 0:65], 65
                K.op(K.pe, lambda: pe.matmul(ps_o.ap[0:M, :], lhsT, p.ap, start=(kb == 0), stop=(kb == nb - 1)),
                     reads=[vb, p], writes=[ps_o])
                if kb + SB < nb:
                    smm(kb + SB)

            def epilogue(h=h, odd=odd, ps_o=ps_o):
                dr = 0 if odd else 64
                ps_b = self.psum[6]
                K.op(K.dve, lambda: dve.reciprocal(rec.ap[dr:dr + 1, :], ps_o.ap[dr:dr + 1, :]), reads=[ps_o], writes=[rec])
                K.op(K.pe, lambda: pe.matmul(ps_b.ap, self.ones32.ap[dr:dr + 1, :], rec.ap[dr:dr + 1, :], start=True, stop=True),
                     reads=[rec, self.ones32], writes=[ps_b])
                bc = bcs[h % 2]
                ob = 64 if odd else 0
                K.op(K.act, lambda: act.copy(bc.ap[ob:ob + 64], ps_b.ap[ob:ob + 64]), reads=[ps_b], writes=[bc])
                K.op(K.dve, lambda: dve.tensor_tensor(self.OT[h // 2].ap[ob:ob + 64], ps_o.ap[ob:ob + 64], bc.ap[ob:ob + 64], ALU.mult),
                     reads=[ps_o, bc], writes=[self.OT[h // 2]])
            pend = epilogue
        pend()

    def dsa_topk(self, j, ab):
        K = self.K
        act, dve, pool, pe = K.act.h, K.dve.h, K.pool.h, K.pe.h
        maskT, mk, qib, kib, wib, bis, bisg = (ab[k] for k in ("maskT", "mk", "qib", "kib", "wib", "bis", "bisg"))
        scs = [ab["sc"], ab["sc1"]]
        nkt = (j + 1) * T
        nbt = 4 * (j + 1)
        K.dma(K.sp, kib.ap[:, 0:nkt], self.KI[:, 0:nkt], reads=[self.KI_buf], writes=[kib])
        NIT = 20
        blocks = []
        for qs in range(4):
            i = 4 * j + qs
            qc = slice(qs * 128, (qs + 1) * 128)
            if i + 1 < nbt:
                K.op(K.pool, lambda: pool.memset(maskT.ap[:, i + 1:nbt, qc], 0.0), writes=[maskT])
            if i < 2:
                if i > 0:
                    K.op(K.pool, lambda: pool.memset(maskT.ap[:, 0:i, qc], 1.0), writes=[maskT])
                K.op(K.pool, lambda: pool.tensor_copy(maskT.ap[:, i, qc], self.cmb.ap[:, 0:128]), reads=[self.cmb], writes=[maskT])
            else:
                blocks.append(qs)

        def scores(qs):
            i = 4 * j + qs
            nk = 128 * (i + 1)
            sc = scs[qs % 2]
            K.dma(K.sp, qib.ap, self.QI[:, :, i * 128:(i + 1) * 128].rearrange("h d t -> d h t"), reads=[self.QI_buf], writes=[qib])
            K.dma(K.sp, wib.ap, self.WI[i * 128:(i + 1) * 128, :], reads=[self.WI_buf], writes=[wib])
            for g in range((nk + 511) // 512):
                n = min(512, nk - g * 512)
                scg = sc.ap[:, g * 512:g * 512 + n]
                for h in range(8):
                    ps = self.psum[2 + h % 2]
                    self.mm(ps, [(qib.ap[:, h, :], kib.ap[:, g * 512:g * 512 + n])], reads=[qib, kib], N=n)
                    r = self.tmp[h % 2]
                    K.op(K.act, lambda: act.activation(r.ap[:, 0:n], ps.ap[:, 0:n], AF.Relu), reads=[ps], writes=[r])
                    if h == 0:
                        K.op(K.pool, lambda: pool.tensor_scalar(scg, r.ap[:, 0:n], wib.ap[:, 0:1], 0.0, ALU.mult, ALU.add),
                             reads=[r, wib], writes=[sc])
                    else:
                        tp = self.tmp[2 + h % 2]
                        K.op(K.pool, lambda: pool.tensor_scalar(tp.ap[:, 0:n], r.ap[:, 0:n], wib.ap[:, h:h + 1], 0.0, ALU.mult, ALU.add),
                             reads=[r, wib], writes=[tp])
                        K.op(K.pool, lambda: pool.tensor_tensor(scg, scg, tp.ap[:, 0:n], ALU.add), reads=[sc, tp], writes=[sc])

        def select(qs):
            i = 4 * j + qs
            nk = 128 * (i + 1)
            qc = slice(qs * 128, (qs + 1) * 128)
            sc = scs[qs % 2]
            scn = sc.ap[:, 0:nk]
            lo, w0, mid, cnt, mx = (bis.ap[:, c:c + 1] for c in range(5))
            K.op(K.dve, lambda: dve.tensor_reduce(mx, scn, AX.X, ALU.max), reads=[sc], writes=[bis])
            K.op(K.dve, lambda: dve.tensor_reduce(lo, scn, AX.X, ALU.min), reads=[sc], writes=[bis])
            K.op(K.dve, lambda: dve.tensor_tensor(sc.ap[:, i * 128:(i + 1) * 128], sc.ap[:, i * 128:(i + 1) * 128], self.cneg.ap, ALU.add),
                 reads=[sc, self.cneg], writes=[sc])
            K.op(K.dve, lambda: dve.tensor_tensor(w0, mx, lo, ALU.subtract), reads=[bis], writes=[bis])
            K.op(K.dve, lambda: dve.tensor_scalar(w0, w0, 1.001, 1e-20, ALU.mult, ALU.add), reads=[bis], writes=[bis])
            for it in range(NIT):
                K.op(K.dve, lambda: dve.tensor_scalar(mid, w0, 2.0 ** (-(it + 1)), lo, ALU.mult, ALU.add), reads=[bis], writes=[bis])
                K.op(K.dve, lambda: dve.tensor_scalar(mk.ap[:, 0:nk], scn, mid, None, ALU.is_ge, ALU.add, accum_out=cnt),
                     reads=[sc, bis], writes=[mk, bis])
                K.op(K.dve, lambda: dve.tensor_scalar(bisg.ap[:, 0:1], cnt, float(TOPK), None, ALU.is_ge), reads=[bis], writes=[bisg])
                K.op(K.dve, lambda: dve.copy_predicated(lo, bisg.ap[:, 0:1], mid), reads=[bis, bisg], writes=[bis])
            K.op(K.dve, lambda: dve.tensor_scalar(mk.ap[:, 0:nk], scn, lo, None, ALU.is_ge), reads=[sc, bis], writes=[mk])
            pst = self.psum[7]
            pstb = pst.ap.bitcast(BF16)
            for kb0 in range(0, i + 1, 4):
                nbk = min(4, i + 1 - kb0)

                def tr():
                    ins = None
                    for t in range(nbk):
                        ins = pe.transpose(pstb[:, t * 128:(t + 1) * 128], mk.ap[:, (kb0 + t) * 128:(kb0 + t + 1) * 128],
                                           self.cmb.ap[:, 2048:2176])
                    return ins
                K.op(K.pe, tr, reads=[mk, self.cmb], writes=[pst])
                K.op(K.act, lambda: act.copy(maskT.ap[:, kb0:kb0 + nbk, qc], pstb[:, 0:nbk * 128].rearrange("p (a t) -> p a t", t=128)),
                     reads=[pst], writes=[maskT])

        if blocks:
            scores(blocks[0])
        for bi, qs in enumerate(blocks):
            if bi + 1 < len(blocks):
                scores(blocks[bi + 1])
            select(qs)

    def phase_b(self, L, dst, dst_buf, last):
        from contextlib import ExitStack
        K, nc = self.K, self.nc
        act, dve, pool = K.act.h, K.dve.h, K.pool.h
        kind = L % 3
        S = self.S
        dk, scale = ((64, 64 ** -0.5), (70, 64 ** -0.5), (96, 96 ** -0.5))[kind]
        with ExitStack() as es:
            sb = lambda name, shape, dtype: es.enter_context(nc.sbuf_tensor(name + "_b%d" % L, shape, dtype)).ap()
            ab = {}
            ab["ktb"] = _arr(sb("ktb", [128, 2, S], BF16), "ktb", 2)
            ab["vev"] = Buf(sb("vev", [128, self.NB, 80], BF16), "vev")
            ab["vod"] = Buf(sb("vod", [128, self.NB, 128], BF16), "vod")
            ab["qtb"] = _arr(sb("qtb", [128, 2, T], BF16), "qtb", 2)
            ab["pT"] = _arr(sb("pT", [128, 4, T], BF16), "pT", 4)
            ab["rec"] = self.stat[0]
            ab["bcs"] = [self.stat[1], self.stat[2]]
            self.OT = _arr(sb("OT", [128, KC, T], BF16), "OT", KC)
            K.op(K.pool, lambda: pool.memset(ab["vev"].ap[:, :, 64:65], 1.0), writes=[ab["vev"]])
            K.op(K.pool, lambda: pool.memset(ab["vod"].ap[:, :, 0:64], 0.0), writes=[ab["vod"]])
            K.op(K.pool, lambda: pool.memset(ab["vod"].ap[:, :, 0:1], 1.0), writes=[ab["vod"]])
            if kind != 0:
                ab["cnT"] = Buf(sb("cnT", [128, 4 * T], F32), "cnT")
                K.dma(K.sp, ab["cnT"].ap, self.cnT_in, writes=[ab["cnT"]])
            if kind == 0:
                ab["maskT"] = Buf(sb("maskT", [128, self.NB, T], BF16), "maskT")
                ab["sc"] = Buf(self.z_full, "sc")
                ab["mk"] = Buf(self.gT_full[:, 0:S], "mk")
                ab["sc1"] = Buf(self.x_full, "sc1")
                ab["sc1"].alias = self.x
                for b_ in self.x:
                    b_.alias = [ab["sc1"]]
                ab["sc"].alias = self.z
                ab["mk"].alias = self.gT[0:(2 * S + T * 2 - 1) // (T * 2)]
                for b in ab["sc"].alias:
                    b.alias = [ab["sc"]]
                for b in ab["mk"].alias:
                    b.alias = [ab["mk"]]
                ab["qib"] = Buf(sb("qib", [64, 8, 128], BF16), "qib")
                ab["kib"] = Buf(sb("kib", [64, S], BF16), "kib")
                ab["wib"] = Buf(sb("wib", [128, 8], F32), "wib")
                ab["bis"] = Buf(sb("bis", [128, 8], F32), "bis")
                ab["bisg"] = Buf(sb("bisg", [128, 2], mybir.dt.uint32), "bisg")
            for j in range(self.NT):
                K.maybe_epoch()
                masked = None
                if kind == 0:
                    self.dsa_topk(j, ab)
                    masked = ab["maskT"]
                self.load_x(self.xs[0], self.xs_buf[0], j, False)
                self.attention(j, dk, scale, masked, ab)
                for m in range(KC):
                    if m % 2 == 0:
                        slot = K.wnext(("wo", L, m // 2), 2048)
                    ps = self.psum[m % 2]
                    off = (m % 2) * 1024
                    self.mm(ps, [(slot.ap[:, off + c * 128: off + (c + 1) * 128], self.OT[c].ap) for c in range(KC)],
                            reads=[slot] + self.OT)
                    K.op(K.dve, lambda: dve.tensor_tensor(self.z[m].ap, ps.ap, self.x[m].ap, ALU.add),
                         reads=[ps, self.x[m]], writes=[self.z[m]])
                self.layernorm(gcol=L * 24 + 8, bcol=96 + L * 24 + 8, scale_after=True)
                self.ffn(L, 1)
                self.layernorm(gcol=L * 24 + 16, bcol=96 + L * 24 + 16, scale_after=False)
                self.store_x(dst, dst_buf, j)
            K.barrier(False)
            if kind == 0:
                for b in self.z + self.gT + self.x:
                    b.alias = []

    def emit(self):
        K, nc = self.K, self.nc
        pool, dve = K.pool.h, K.dve.h
        CH = 128 * 8192
        for i in range(self.wtotal // CH):
            K.dma(K.pool, K.wbf[i * CH:(i + 1) * CH].rearrange("(p n) -> p n", p=128),
                  self.wsrc[i * CH:(i + 1) * CH].rearrange("(p n) -> p n", p=128), reads=[], writes=[K.wbf_bufs[i]])
        K.dma(K.sp, self.prm_sb.ap, self.prm, writes=[self.prm_sb])
        K.dma(K.sp, self.cmb.ap, self.cmb_in, writes=[self.cmb])
        K.dma(K.sp, self.cneg.ap, self.cneg_in, writes=[self.cneg])
        K.op(K.pool, lambda: pool.memset(self.ones32.ap, 1.0), writes=[self.ones32])
        K.op(K.dve, lambda: dve.tensor_scalar(self.prm2_sb.ap, self.prm_sb.ap, ALPHA, None, ALU.mult), reads=[self.prm_sb], writes=[self.prm2_sb])
        for b in (self.ones32, self.cmb, self.cneg, self.prm_sb, self.prm2_sb):
            b.const = True
        for b in K.wbf_bufs:
            b.const = True
        if self.start_barrier:
            K.barrier(False)

        src, src_buf = self.xin, self.xin_buf
        for li, L in enumerate(self.layers):
            last = (li == len(self.layers) - 1)
            if self.stop_after == "a" and last:
                self.phase_a(L, src, src_buf)
                for j in range(self.NT):
                    self.load_x(self.xs[0], self.xs_buf[0], j, False)
                    self.store_x(self.xout, self.xout_buf, j)
                break
            self.phase_a(L, src, src_buf)
            if last:
                self.phase_b(L, self.xout, self.xout_buf, True)
            else:
                self.phase_b(L, self.xs[1], self.xs_buf[1], False)
                src, src_buf = self.xs[1], self.xs_buf[1]
        K.barrier(False)


def _wt(W, cols, nk):
    cols = np.asarray(cols)
    return W[:nk * 128][:, cols].reshape(nk, 128, len(cols)).transpose(1, 0, 2)


def _swap_idx(base, width, rot):
    idx = np.arange(base, base + width)
    h = rot // 2
    idx[:h] = np.arange(base + h, base + rot)
    idx[h:rot] = np.arange(base, base + h)
    return idx


def host_block(key, inp):
    kind = key[0]
    if kind == "w13":
        _, L, which, cp = key
        w = inp["ffn1_w13" if which == 0 else "ffn2_w13"][L]
        out = np.empty((128, 2, 2, KC, 128), np.float32)
        for c2 in range(2):
            c = cp * 2 + c2
            for gu in range(2):
                out[:, c2, gu] = _wt(w, np.arange(gu * DFF + c * 128, gu * DFF + (c + 1) * 128), KC)
        return out.reshape(128, -1)
    if kind == "w2":
        _, L, which, m = key
        w = inp["ffn1_w2" if which == 0 else "ffn2_w2"][L]
        return _wt(w, np.arange(m * 128, (m + 1) * 128), FC).reshape(128, -1)
    if kind == "wo":
        _, L, mp = key
        w = inp["w_out"][L]
        out = np.empty((128, 2, KC, 128), np.float32)
        for m2 in range(2):
            m = mp * 2 + m2
            out[:, m2] = _wt(w, np.arange(m * 128, (m + 1) * 128), KC)
        return out.reshape(128, -1)
    L = key[1]
    mk = L % 3
    if mk == 0:
        w = inp["dsa_w_in"][L // 3]
    elif mk == 1:
        w = inp["fox_w_in"][L // 3]
    else:
        w = inp["mla_w_dqkv"][L // 3]
    if kind == "pj" and mk == 0:
        gs = key[2]
        out = np.zeros((128, 2, 2, KC, 128), np.float32)
        for g2 in range(2):
            g = gs * 2 + g2
            if g > 20:
                continue
            if g < 8:
                bases = [g * 128, g * 128 + 64]
            elif g < 16:
                bases = [1024 + (g - 8) * 128, 1024 + (g - 8) * 128 + 64]
            elif g < 20:
                bases = [3072 + (g - 16) * 128, 3072 + (g - 16) * 128 + 64]
            else:
                bases = [3584, 3584]
            main = np.concatenate([np.arange(b0, b0 + 64) for b0 in bases])
            swap = np.concatenate([_swap_idx(b0, 64, 16) for b0 in bases])
            out[:, g2, 0] = _wt(w, main, KC)
            out[:, g2, 1] = _wt(w, swap, KC)
        return out.reshape(128, -1)
    if kind == "pj" and mk == 1:
        gs = key[2]
        out = np.zeros((128, 4, KC, 128), np.float32)
        for g4 in range(4):
            g = gs * 4 + g4
            base = g * 128 if g < 8 else 1024 + (g - 8) * 128
            out[:, g4] = _wt(w, np.arange(base, base + 128), KC)
        return out.reshape(128, -1)
    if kind == "pj" and mk == 2:
        gs = key[2]
        out = np.zeros((128, 4, KC, 128), np.float32)
        for g4 in range(4):
            g = gs * 4 + g4
            if g < 5:
                out[:, g4] = _wt(w, np.arange(g * 128, (g + 1) * 128), KC)
        return out.reshape(128, -1)
    if kind == "pf":
        return _wt(w, np.arange(3072, 3088), KC).reshape(128, -1)
    if kind == "pkr":
        out = np.zeros((128, 2, KC, 96), np.float32)
        out[:, 0, :, 64:96] = _wt(w, np.arange(640, 672), KC)
        out[:, 1, :, 64:96] = _wt(w, _swap_idx(640, 32, 32), KC)
        return out.reshape(128, -1)
    if kind == "pv":
        half = key[2]
        return _wt(w, np.arange(2048 + half * 512, 2048 + (half + 1) * 512), KC).reshape(128, -1)
    if kind == "pw":
        return _wt(w, np.arange(3648, 3656), KC).reshape(128, -1)
    if kind == "uq":
        hq = key[2]
        wq = inp["mla_w_uq"][L // 3]
        out = np.zeros((128, 4, 2, 3, 96), np.float32)
        for h4 in range(4):
            h = hq * 4 + h4
            main = np.arange(h * 96, (h + 1) * 96)
            swap = np.concatenate([np.arange(h * 96, h * 96 + 64), _swap_idx(h * 96 + 64, 32, 32)])
            out[:, h4, 0] = _wt(wq, main, 3)
            out[:, h4, 1] = _wt(wq, swap, 3)
        return out.reshape(128, -1)
    if kind == "uk":
        wk = inp["mla_w_ukv"][L // 3]
        out = np.zeros((128, 8, 2, 128), np.float32)
        for i in range(8):
            cols = np.concatenate([np.arange(2 * i * 128, 2 * i * 128 + 64), np.arange((2 * i + 1) * 128, (2 * i + 1) * 128 + 64)])
            out[:, i] = _wt(wk, cols, 2)
        return out.reshape(128, -1)
    if kind == "uv":
        wk = inp["mla_w_ukv"][L // 3]
        cols = np.concatenate([np.arange(h * 128 + 64, h * 128 + 128) for h in range(H)])
        return _wt(wk, cols, 2).reshape(128, -1)
    raise KeyError(key)


def host_consts(S):
    import ml_dtypes
    f32 = np.float32
    tabs = np.zeros((4, 128, S), f32)
    pos = np.arange(S, dtype=f32)
    inv = (f32(ROPE_THETA) ** (-np.arange(0, 16, 2, dtype=f32) / f32(16))).astype(f32)
    ang = (pos[None, :] * inv[:, None]).astype(f32)
    c, s_ = np.cos(ang).astype(f32), np.sin(ang).astype(f32)
    tabs[0] = 1.0
    for hb in (0, 64):
        tabs[0, hb:hb + 8] = c
        tabs[0, hb + 8:hb + 16] = c
        tabs[1, hb:hb + 8] = -s_
        tabs[1, hb + 8:hb + 16] = s_
    invm = (f32(ROPE_THETA) ** (-np.arange(0, 32, 2, dtype=f32) / f32(32))).astype(f32)
    angm = (pos[None, :] * invm[:, None]).astype(f32)
    cm_, sm_ = np.cos(angm).astype(f32), np.sin(angm).astype(f32)
    tabs[2] = 1.0
    tabs[2, 64:80] = cm_
    tabs[2, 80:96] = cm_
    tabs[3, 64:80] = -sm_
    tabs[3, 80:96] = sm_
    cmb = np.zeros((128, 4 * 512 + 128), f32)
    p = np.arange(128)[:, None]
    q = np.arange(512)[None, :]
    for a in range(4):
        cmb[:, a * 512:(a + 1) * 512] = ((a * 128 + p) <= q)
    cmb[:, 2048:2176] = np.eye(128, dtype=f32)
    cneg = np.where(np.arange(128)[None, :] <= p, 0.0, NEG).astype(f32)
    cnT = np.where(cmb[:, 0:2048] > 0, 0.0, NEG).astype(f32)
    return tabs, cmb.astype(ml_dtypes.bfloat16), cneg, cnT


_CACHE = {}


def get_prog(S, layers, stop_after=None):
    key = (S, tuple(layers), stop_after)
    if key not in _CACHE:
        pl = Prog(S, layers, True, stop_after=stop_after).plan
        pr = Prog(S, layers, False, plan=pl, stop_after=stop_after)
        _CACHE[key] = (pl, pr)
    return _CACHE[key]


def run(inputs, S=4096, layers=(0, 1, 2, 3), n_cores=8, trace=False, stop_after=None):
    plan, prog = get_prog(S, layers, stop_after)
    inp = {k: np.asarray(v) for k, v in inputs.items()}
    wsrc = np.zeros((plan["wtotal"],), np.float32)
    for key, (off, n) in plan["wblocks"].items():
        wsrc[off:off + 128 * n] = host_block(key, inp).reshape(-1)
    prm = np.zeros((128, 256), np.float32)
    prm[:, 0:96] = inp["ln_g"].reshape(DEPTH * 3 * KC, 128).T
    prm[:, 96:192] = inp["ln_b"].reshape(DEPTH * 3 * KC, 128).T
    prm[0:16, 192] = inp["fox_b_f"][0]
    prm[:, 193:196] = inp["mla_q_norm_g"][0].reshape(3, 128).T
    prm[:, 196:198] = inp["mla_kv_norm_g"][0].reshape(2, 128).T
    tabs, cmb, cneg, cnT = host_consts(S)
    x = inp["x"]
    in_maps = []
    for c in range(n_cores):
        in_maps.append({"xT": np.ascontiguousarray(x[c, :S].T), "wsrc": wsrc, "prm": prm, "tabs": tabs, "cmb_in": cmb, "cneg_in": cneg, "cnT_in": cnT})
    res = run_bass_kernel_spmd(prog.nc, in_maps, core_ids=list(range(n_cores)), trace=trace)
    out = np.stack([res.results[c]["outT"].T for c in range(n_cores)], axis=0)
    return out, res


def kernel(**inputs):
    out, _ = run(inputs)
    return np.ascontiguousarray(out.astype(np.float32))
```

```python
import numpy as np
import concourse.bass as bass
import concourse.mybir as mybir
from concourse.bass_utils import run_bass_kernel_spmd

F32 = mybir.dt.float32
BF16 = mybir.dt.bfloat16
AF = mybir.ActivationFunctionType
ALU = mybir.AluOpType
AX = mybir.AxisListType

D = 1024
KC = 8
T = 512
DFF = 2816
FC = 22
H = 16
DEPTH = 4
ALPHA = (2.0 * DEPTH) ** 0.25
LN_EPS = 1e-5
RMS_EPS = 1e-6
TOPK = 256
SLOT = 4096
NSLOT = 4
ROPE_THETA = 500000.0
NEG = -1.0e30


class Sem:
    def __init__(self, h, sid):
        self.h = h
        self.id = sid
        self.dead = False


class Eng:
    def __init__(self, name, h, compute):
        self.name = name
        self.h = h
        self.compute = compute
        self.sem = None
        self.count = 0
        self.waited = {}


class Buf:
    def __init__(self, ap, name="", const=False, untracked=False):
        self.ap = ap
        self.name = name
        self.const = const
        self.untracked = untracked
        self.w = None
        self.r = {}
        self.alias = []


def _exp(bufs):
    out = []
    for b in bufs:
        out.append(b)
        out.extend(b.alias)
    return out


class KB:
    def __init__(self, nc, planning):
        self.nc = nc
        self.planning = planning
        self.nsem = 0
        self.pe = Eng("pe", nc.tensor, True)
        self.act = Eng("act", nc.scalar, True)
        self.dve = Eng("dve", nc.vector, True)
        self.pool = Eng("pool", nc.gpsimd, True)
        self.sp = Eng("sp", nc.sync, False)
        self.engines = [self.pe, self.act, self.dve, self.pool, self.sp]
        self.dma_pool = []
        self.dma_rr = 0
        self.wplan = []
        self.wblocks = {}
        self.woff = 0
        self.wuse = 0
        self.wloaded = 0
        self.wseq = None
        self.ninstr = 0
        if not planning:
            for e in self.engines:
                if e.compute:
                    e.sem = self.new_sem(e.name)
            self.dma_pools = {}
            for q in (self.sp, self.pool, self.act):
                self.dma_pools[q.name] = [[self.new_sem("dma%s%d" % (q.name, i)), 0] for i in range({"sp": 12, "pool": 4, "act": 12}[q.name])]
            self.dma_rrs = {"sp": 0, "pool": 0, "act": 0}

    def new_sem(self, name):
        self.nsem += 1
        h = self.nc.alloc_semaphore(name="%s_%d" % (name, self.nsem))
        return Sem(h, self.nsem)

    def _wait(self, eng, tok):
        sem, val = tok
        if sem.dead:
            return
        if eng.waited.get(sem.id, 0) >= val:
            return
        eng.h.wait_ge(sem.h, val)
        eng.waited[sem.id] = val
        self.ninstr += 1

    def _deps(self, eng, reads, writes):
        reads = [b for b in _exp(reads) if not b.untracked]
        writes = [b for b in _exp(writes) if not b.untracked]
        skip_same = eng is self.pe
        for b in reads:
            if b.w is not None and not (skip_same and b.w[0] is eng.sem):
                self._wait(eng, b.w)
        for b in writes:
            if b.w is not None and not (skip_same and b.w[0] is eng.sem):
                self._wait(eng, b.w)
            for tok in b.r.values():
                if not (skip_same and tok[0] is eng.sem):
                    self._wait(eng, tok)

    def _mark(self, tok, reads, writes):
        reads = [b for b in reads if not b.untracked]
        writes = [b for b in writes if not b.untracked]
        for b in reads:
            if not b.const:
                b.r[tok[0].id] = tok
        for b in writes:
            b.w = tok
            b.r = {}

    def op(self, eng, fn, reads=(), writes=()):
        if self.planning:
            return
        self._deps(eng, reads, writes)
        ins = fn()
        eng.count += 1
        ins.then_inc(eng.sem.h, 1)
        self.ninstr += 1
        self._mark((eng.sem, eng.count), reads, writes)

    def dma(self, q, out_ap, in_ap, reads=(), writes=()):
        if self.planning:
            return
        self._deps(q, reads, writes)
        pl = self.dma_pools[q.name]
        ent = pl[self.dma_rrs[q.name]]
        self.dma_rrs[q.name] = (self.dma_rrs[q.name] + 1) % len(pl)
        if ent[1] > 0:
            self._wait(q, (ent[0], ent[1]))
        ent[1] += 16
        q.h.dma_start(out=out_ap, in_=in_ap).then_inc(ent[0].h, 16)
        self.ninstr += 1
        self._mark((ent[0], ent[1]), reads, writes)

    def barrier(self, new_epoch=True):
        if self.planning:
            return
        toks = [(e.sem, e.count) for e in self.engines if e.compute and e.count > 0]
        toks += [(ent[0], ent[1]) for pl in self.dma_pools.values() for ent in pl if ent[1] > 0]
        for e in self.engines:
            for tok in toks:
                if tok[0] is not e.sem:
                    self._wait(e, tok)
        if new_epoch:
            for e in self.engines:
                if e.compute:
                    e.sem.dead = True
                    e.sem = self.new_sem(e.name)
                    e.count = 0

    def maybe_epoch(self):
        if self.planning:
            return
        if max(e.count for e in self.engines if e.compute) > 9000:
            self.barrier(True)

    def wnext(self, key, n, hold=0):
        if self.planning:
            if key not in self.wblocks:
                self.wblocks[key] = (self.woff, n)
                self.woff += 128 * n
            self.wplan.append(key)
            return self.slots[0]
        u = self.wuse
        assert self.wseq[u] == key, (self.wseq[u], key)
        while self.wloaded < len(self.wseq) and self.wloaded < u + NSLOT - hold:
            k2 = self.wseq[self.wloaded]
            off, n2 = self.wblocks[k2]
            sl = self.slots[self.wloaded % NSLOT]
            src = self.wbf[off:off + 128 * n2].rearrange("(p n) -> p n", p=128)
            CH = 128 * 8192
            rb = self.wbf_bufs[off // CH:(off + 128 * n2 - 1) // CH + 1]
            self.dma(self.sp, sl.ap[:, 0:n2], src, reads=rb, writes=[sl])
            self.wloaded += 1
        self.wuse += 1
        return self.slots[u % NSLOT]


def _arr(ap3, name, n):
    return [Buf(ap3[:, i, :], "%s%d" % (name, i)) for i in range(n)]


class Prog:
    def __init__(self, S, layers, planning, plan=None, stop_after=None):
        self.S = S
        self.NT = S // T
        self.NB = S // 128
        self.layers = layers
        self.stop_after = stop_after
        import os
        self.start_barrier = os.environ.get("KB_START_BARRIER", "1") == "1"
        nc = bass.Bass("TRN2", target_bir_lowering=False)
        self.nc = nc
        K = KB(nc, planning)
        self.K = K
        if plan is not None:
            K.wseq = plan["wseq"]
            K.wblocks = plan["wblocks"]
            self.wtotal = plan["wtotal"]
        else:
            self.wtotal = 128 * SLOT
        self.alloc()
        self.emit()
        if planning:
            tot = K.woff
            pad = (-tot) % (128 * 8192)
            self.plan = {"wseq": K.wplan, "wblocks": K.wblocks, "wtotal": tot + pad}

    def alloc(self):
        nc, K, S = self.nc, self.K, self.S
        dt = nc.dram_tensor
        self.xin = dt("xT", [D, S], F32, kind="ExternalInput").ap()
        self.wsrc = dt("wsrc", [self.wtotal], F32, kind="ExternalInput").ap()
        self.prm = dt("prm", [128, 256], F32, kind="ExternalInput").ap()
        self.tabs = dt("tabs", [4, 128, S], F32, kind="ExternalInput").ap()
        self.cmb_in = dt("cmb_in", [128, 4 * 512 + 128], BF16, kind="ExternalInput").ap()
        self.cneg_in = dt("cneg_in", [128, 128], F32, kind="ExternalInput").ap()
        self.cnT_in = dt("cnT_in", [128, 4 * T], F32, kind="ExternalInput").ap()
        self.xout = dt("outT", [D, S], F32, kind="ExternalOutput").ap()
        K.wbf = dt("wbf", [self.wtotal], BF16).ap()
        K.wbf_bufs = [Buf(K.wbf, "wbf%d" % i) for i in range(self.wtotal // (128 * 8192))]
        self.xs = [dt("xs%d" % i, [D, S], F32).ap() for i in range(2)]
        self.xs_buf = [Buf(self.xs[i], "xs%d" % i, untracked=True) for i in range(2)]
        self.QT = dt("QT", [H, 128, S], BF16).ap()
        self.KT = dt("KT", [H, 128, S], BF16).ap()
        self.VS = dt("VS", [H, 128, self.NB, 64], BF16).ap()
        self.QI = dt("QI", [8, 64, S], BF16).ap()
        self.KI = dt("KI", [64, S], BF16).ap()
        self.WI = dt("WI", [S, 8], F32).ap()
        self.QT_buf, self.KT_buf, self.VS_buf = Buf(self.QT, "QT", untracked=True), Buf(self.KT, "KT", untracked=True), Buf(self.VS, "VS", untracked=True)
        self.QI_buf, self.KI_buf, self.WI_buf = Buf(self.QI, "QI", untracked=True), Buf(self.KI, "KI", untracked=True), Buf(self.WI, "WI", untracked=True)
        self.xin_buf = Buf(self.xin, "xin", const=True)
        self.xout_buf = Buf(self.xout, "xout", untracked=True)

        sb = lambda name, shape, dtype: nc.alloc_sbuf_tensor(name, shape, dtype).ap()
        ring = sb("ring", [128, NSLOT, SLOT], BF16)
        K.slots = [Buf(ring[:, i, :], "slot%d" % i) for i in range(NSLOT)]
        xt_ = sb("x", [128, KC, T], F32)
        self.x = _arr(xt_, "x", KC)
        self.x_full = xt_.rearrange("p a t -> p (a t)")
        self.xb = _arr(sb("xb", [128, KC, T], BF16), "xb", KC)
        zt = sb("z", [128, KC, T], F32)
        self.z = _arr(zt, "z", KC)
        self.z_full = zt.rearrange("p a t -> p (a t)")
        gt = sb("gT", [128, FC, T], BF16)
        self.gT = _arr(gt, "gT", FC)
        self.gT_full = gt.rearrange("p a t -> p (a t)")
        self.tmp = _arr(sb("tmp", [128, 4, T], F32), "tmp", 4)
        self.stat = _arr(sb("stat", [128, 4, T], F32), "stat", 4)
        self.prm_sb = Buf(sb("prm_sb", [128, 256], F32), "prm_sb")
        self.prm2_sb = Buf(sb("prm2_sb", [128, 256], F32), "prm2_sb")
        self.ones32 = Buf(sb("ones32", [128, 128], F32), "ones32")
        self.cmb = Buf(sb("cmb", [128, 4 * 512 + 128], BF16), "cmb")
        self.cneg = Buf(sb("cneg", [128, 128], F32), "cneg")
        self.psum = [Buf(nc.alloc_psum_tensor("ps%d" % i, [128, 512], F32).ap(), "ps%d" % i) for i in range(8)]

    def mm(self, ps, pairs, reads, M=128, N=T, pbase=0):
        K = self.K
        pe = K.pe.h
        out = ps.ap[pbase:pbase + M, 0:N]

        def f():
            ins = None
            n = len(pairs)
            for i, (l, r) in enumerate(pairs):
                ins = pe.matmul(out, l, r, start=(i == 0), stop=(i == n - 1))
            K.ninstr += n - 1
            return ins
        K.op(K.pe, f, reads=reads, writes=[ps])

    def prmcol(self, c, n=128):
        return self.prm_sb.ap[0:n, c:c + 1]

    def ffn(self, L, which, side=None):
        K = self.K
        act, dve = K.act.h, K.dve.h
        xb, gT, z, x = self.xb, self.gT, self.z, self.x
        for cp in range(FC // 2):
            slot = K.wnext(("w13", L, which, cp), 4096)
            for c2 in range(2):
                c = cp * 2 + c2
                psg, psu = self.psum[(c % 2) * 2], self.psum[(c % 2) * 2 + 1]
                for gu, ps in ((0, psg), (1, psu)):
                    base = (c2 * 2 + gu) * KC
                    self.mm(ps, [(slot.ap[:, (base + k) * 128:(base + k + 1) * 128], xb[k].ap) for k in range(KC)],
                            reads=[slot] + xb)
                sg = self.tmp[c % 2]
                K.op(K.act, lambda: act.activation(sg.ap, psg.ap, AF.Silu), reads=[psg], writes=[sg])
                K.op(K.dve, lambda: dve.tensor_tensor(gT[c].ap, psu.ap, sg.ap, ALU.mult), reads=[psu, sg], writes=[gT[c]])
                if side:
                    side.pop(0)()
        for m in range(KC):
            slot = K.wnext(("w2", L, which, m), FC * 128)
            ps = self.psum[4 + m % 2]
            self.mm(ps, [(slot.ap[:, c * 128:(c + 1) * 128], gT[c].ap) for c in range(FC)], reads=[slot] + gT)
            K.op(K.dve, lambda: dve.scalar_tensor_tensor(z[m].ap, ps.ap, 0.5, x[m].ap, ALU.mult, ALU.add),
                 reads=[ps, x[m]], writes=[z[m]])

    def layernorm(self, gcol, bcol, scale_after, defer=False, tbufs=None, after=None):
        K = self.K
        act, dve, pool, pe = K.act.h, K.dve.h, K.pool.h, K.pe.h
        z, x, xb = self.z, self.x, self.xb
        ps_s, ps_q = self.psum[4], self.psum[5]
        mean, var, std, rstd = self.stat
        self.mm(ps_s, [(self.ones32.ap, z[m].ap) for m in range(KC)], reads=z + [self.ones32])
        for m in range(KC):
            sq = self.tmp[2 + m % 2]
            K.op(K.act, lambda: act.activation(sq.ap, z[m].ap, AF.Square), reads=[z[m]], writes=[sq])
            K.op(K.pe, lambda: pe.matmul(ps_q.ap, self.ones32.ap, sq.ap, start=(m == 0), stop=(m == KC - 1)),
                 reads=[sq, self.ones32], writes=[ps_q])
        K.op(K.dve, lambda: dve.tensor_scalar(mean.ap, ps_s.ap, 1.0 / D, None, ALU.mult), reads=[ps_s], writes=[mean])
        K.op(K.pool, lambda: pool.tensor_tensor(var.ap, mean.ap, mean.ap, ALU.mult), reads=[mean], writes=[var])
        K.op(K.dve, lambda: dve.scalar_tensor_tensor(var.ap, ps_q.ap, 1.0 / D, var.ap, ALU.mult, ALU.subtract),
             reads=[ps_q, var], writes=[var])
        K.op(K.dve, lambda: dve.tensor_scalar(var.ap, var.ap, LN_EPS, None, ALU.add), reads=[var], writes=[var])
        K.op(K.act, lambda: act.activation(std.ap, var.ap, AF.Sqrt), reads=[var], writes=[std])
        K.op(K.dve, lambda: dve.reciprocal(rstd.ap, std.ap), reads=[std], writes=[rstd])
        K.op(K.dve, lambda: dve.scalar_tensor_tensor(var.ap, mean.ap, -1.0, rstd.ap, ALU.mult, ALU.mult), reads=[mean, rstd], writes=[var])
        pb = self.prm2_sb if scale_after else self.prm_sb
        tbuf = tbufs if tbufs is not None else self.tmp
        jobs = []
        for m in range(KC):
            def job(m=m, t=tbuf[m % 2], zm=z[m], xm=x[m], xbm=xb[m]):
                K.op(K.dve, lambda: dve.tensor_tensor(t.ap, zm.ap, rstd.ap, ALU.mult), reads=[zm, rstd], writes=[t])
                K.op(K.pool, lambda: pool.tensor_tensor(t.ap, t.ap, var.ap, ALU.add), reads=[t, var], writes=[t])
                K.op(K.act, lambda: act.activation(xm.ap, t.ap, AF.Identity, bias=pb.ap[:, bcol + m:bcol + m + 1], scale=pb.ap[:, gcol + m:gcol + m + 1]),
                     reads=[t, pb], writes=[xm])
                K.op(K.dve, lambda: dve.tensor_scalar(xbm.ap, t.ap, self.prmcol(gcol + m), self.prmcol(bcol + m), ALU.mult, ALU.add),
                     reads=[t, self.prm_sb], writes=[xbm])
                if after is not None:
                    after(m, xm)
            jobs.append(job)
        if defer:
            return jobs
        for jb in jobs:
            jb()
        return []

    def load_x(self, src, src_buf, j, want_xb, q=None, do_dma=True, do_prep=True):
        K = self.K
        pool = K.pool.h
        q = q or K.sp
        if do_dma:
            for m in range(KC):
                K.dma(q, self.x[m].ap, src[m * 128:(m + 1) * 128, j * T:(j + 1) * T], reads=[src_buf], writes=[self.x[m]])
        if do_prep:
            for m in range(KC):
                if want_xb:
                    K.op(K.pool, lambda: pool.tensor_copy(self.xb[m].ap, self.x[m].ap), reads=[self.x[m]], writes=[self.xb[m]])
                K.op(K.pool, lambda: pool.tensor_scalar(self.x[m].ap, self.x[m].ap, ALPHA, 0.0, ALU.mult, ALU.add),
                     reads=[self.x[m]], writes=[self.x[m]])

    def store_x(self, dst, dst_buf, j):
        K = self.K
        for m in range(KC):
            K.dma(K.act, dst[m * 128:(m + 1) * 128, j * T:(j + 1) * T], self.x[m].ap, reads=[self.x[m]], writes=[dst_buf])

    def proj_group(self, slot, off_main, off_swap, rhs, nk, M, ps_main, ps_swap):
        self.mm(ps_main, [(slot.ap[:, off_main + k * M: off_main + (k + 1) * M], rhs[k].ap) for k in range(nk)],
                reads=[slot] + rhs[:nk], M=M)
        if off_swap is not None:
            self.mm(ps_swap, [(slot.ap[:, off_swap + k * M: off_swap + (k + 1) * M], rhs[k].ap) for k in range(nk)],
                    reads=[slot] + rhs[:nk], M=M)

    def rope_evac(self, ps_main, ps_swap, M, tC, tS, dst, par=0):
        K = self.K
        dve, pool = K.dve.h, K.pool.h
        t1, t2 = self.tmp[(par % 2) * 2], self.tmp[(par % 2) * 2 + 1]
        K.op(K.dve, lambda: dve.tensor_tensor(t1.ap[0:M], ps_main.ap[0:M], tC.ap[0:M], ALU.mult), reads=[ps_main, tC], writes=[t1])
        K.op(K.dve, lambda: dve.tensor_tensor(t2.ap[0:M], ps_swap.ap[0:M], tS.ap[0:M], ALU.mult), reads=[ps_swap, tS], writes=[t2])
        K.op(K.pool, lambda: pool.tensor_tensor(dst.ap[0:M], t1.ap[0:M], t2.ap[0:M], ALU.add), reads=[t1, t2], writes=[dst])

    def vproj(self, L, j, key, src, nk, bufs):
        K = self.K
        act = K.act.h
        if nk * 1024 > SLOT:
            s0 = K.wnext((key, L, 0), nk * 512)
            s1 = K.wnext((key, L, 1), nk * 512, hold=1)
            wap = lambda k, half: (s0, s1)[half].ap[:, k * 512:(k + 1) * 512]
            sl = [s0, s1]
        else:
            s0 = K.wnext((key, L), nk * 1024)
            wap = lambda k, half: s0.ap[:, k * 1024 + half * 512: k * 1024 + (half + 1) * 512]
            sl = [s0]
        for tb in range(4):
            vst = bufs["vst"][tb % 2]
            for half in range(2):
                ps = self.psum[half]
                self.mm(ps, [(src[k].ap[:, tb * 128:(tb + 1) * 128], wap(k, half)) for k in range(nk)], reads=sl + src[:nk])
                K.op(K.act, lambda: act.copy(vst.ap[:, half * 512:(half + 1) * 512], ps.ap), reads=[ps], writes=[vst])
            nb = 4 * j + tb
            K.dma(K.act, self.VS[:, :, nb, :].rearrange("h p d -> p h d"), vst.ap.rearrange("p (h d) -> p h d", d=64),
                  reads=[vst], writes=[self.VS_buf])

    def store_heads(self, dst, dst_buf, stage, i, j, rows=64):
        K = self.K
        K.dma(K.act, dst[2 * i, 0:rows, j * T:(j + 1) * T], stage.ap[0:rows], reads=[stage], writes=[dst_buf])
        K.dma(K.act, dst[2 * i + 1, 0:rows, j * T:(j + 1) * T], stage.ap[64:64 + rows], reads=[stage], writes=[dst_buf])

    def phase_a(self, L, src, src_buf):
        from contextlib import ExitStack
        K, nc = self.K, self.nc
        act, dve, pool = K.act.h, K.dve.h, K.pool.h
        kind = L % 3
        with ExitStack() as es:
            sb = lambda name, shape, dtype: es.enter_context(nc.sbuf_tensor(name + "_a%d" % L, shape, dtype)).ap()
            bufs = {}
            rt = _arr(sb("rt", [128, 2, T], F32), "rt", 2)
            stage = _arr(sb("stage", [128, 2, T], BF16), "stage", 2)
            bufs["vst"] = _arr(sb("vst", [128, 2, 1024], BF16), "vst", 2)
            if kind == 0:
                wst = _arr(sb("wst", [128, 2, 8], F32), "wst", 2)
            if kind == 1:
                cum = _arr(sb("cum", [16, 4, T], F32), "cum", 4)
                cons = Buf(sb("cons", [16, T], F32), "cons")
                cst = Buf(sb("cst", [16, 6, T], BF16), "cst")
                cst1 = Buf(sb("cst1", [16, 3, T], BF16), "cst1")
                carry = Buf(sb("carry", [16, 2], F32), "carry")
                negb = Buf(sb("negb", [16, 1], F32), "negb")
                K.op(K.pool, lambda: pool.memset(cons.ap, 1.0), writes=[cons])
                K.op(K.pool, lambda: pool.memset(cst1.ap, 1.0), writes=[cst1])
                K.op(K.pool, lambda: pool.memset(carry.ap, 0.0), writes=[carry])
                K.op(K.dve, lambda: dve.tensor_scalar(negb.ap, self.prm_sb.ap[0:16, 192:193], -1.0, None, ALU.mult),
                     reads=[self.prm_sb], writes=[negb])
            if kind == 2:
                lat = _arr(sb("lat", [128, 5, T], F32), "lat", 5)
                latn = _arr(sb("latn", [128, 5, T], BF16), "latn", 5)

            x2t = sb("x2", [128, KC, T], F32)
            xb2t = sb("xb2", [128, KC, T], BF16)
            z2t = sb("z2", [128, KC, T], F32)
            sets = [(self.x, self.xb, self.z), (_arr(x2t, "x2_", KC), _arr(xb2t, "xb2_", KC), _arr(z2t, "z2_", KC))]

            def set_par(p):
                self.x, self.xb, self.z = sets[p % 2]

            def proj(j):
                cols = slice(j * T, (j + 1) * T)
                xb = self.xb
                if kind in (0, 2):
                    tb0 = 0 if kind == 0 else 2
                    for t in range(2):
                        K.dma(K.sp, rt[t].ap, self.tabs[tb0 + t, :, cols], reads=[], writes=[rt[t]])
                if kind == 0:
                    for g in range(21):
                        if g % 2 == 0:
                            slot = K.wnext(("pj", L, g // 2), 4096)
                        off = (g % 2) * 2048
                        pm, psw = self.psum[(g % 3) * 2], self.psum[(g % 3) * 2 + 1]
                        self.proj_group(slot, off, off + 1024, xb, KC, 128, pm, psw)
                        st = stage[g % 2]
                        self.rope_evac(pm, psw, 128, rt[0], rt[1], st, par=g)
                        if g < 8:
                            self.store_heads(self.QT, self.QT_buf, st, g, j)
                        elif g < 16:
                            self.store_heads(self.KT, self.KT_buf, st, g - 8, j)
                        elif g < 20:
                            self.store_heads(self.QI, self.QI_buf, st, g - 16, j)
                        else:
                            K.dma(K.act, self.KI[:, cols], st.ap[0:64], reads=[st], writes=[self.KI_buf])
                    self.vproj(L, j, "pv", xb, KC, bufs)
                    slot = K.wnext(("pw", L), 64)
                    for tb in range(4):
                        ps = self.psum[2]
                        self.mm(ps, [(xb[k].ap[:, tb * 128:(tb + 1) * 128], slot.ap[:, k * 8:(k + 1) * 8]) for k in range(KC)],
                                reads=[slot] + xb, N=8)
                        w = wst[tb % 2]
                        K.op(K.dve, lambda: dve.tensor_copy(w.ap, ps.ap[:, 0:8]), reads=[ps], writes=[w])
                        K.dma(K.act, self.WI[j * T + tb * 128: j * T + (tb + 1) * 128, :], w.ap, reads=[w], writes=[self.WI_buf])
                elif kind == 1:
                    for g in range(16):
                        if g % 4 == 0:
                            slot = K.wnext(("pj", L, g // 4), 4096)
                        pm = self.psum[g % 4]
                        self.proj_group(slot, (g % 4) * 1024, None, xb, KC, 128, pm, None)
                        st = stage[g % 2]
                        K.op(K.act, lambda: act.copy(st.ap, pm.ap), reads=[pm], writes=[st])
                        if g < 8:
                            self.store_heads(self.QT, self.QT_buf, st, g, j)
                        else:
                            self.store_heads(self.KT, self.KT_buf, st, g - 8, j)
                    slot = K.wnext(("pf", L), 128)
                    pm = self.psum[0]
                    self.proj_group(slot, 0, None, xb, KC, 16, pm, None)
                    e, l1, r1, r2 = cum
                    K.op(K.act, lambda: act.activation(e.ap, pm.ap[0:16], AF.Exp, bias=negb.ap, scale=-1.0), reads=[pm, negb], writes=[e])
                    K.op(K.act, lambda: act.activation(l1.ap, e.ap, AF.Ln, bias=1.0, scale=1.0), reads=[e], writes=[l1])
                    K.op(K.dve, lambda: dve.tensor_tensor_scan(e.ap, cons.ap, l1.ap, carry.ap[:, 0:1], ALU.mult, ALU.add),
                         reads=[cons, l1, carry], writes=[e])
                    K.op(K.dve, lambda: dve.tensor_copy(carry.ap[:, 0:1], e.ap[:, T - 1:T]), reads=[e], writes=[carry])
                    K.op(K.dve, lambda: dve.tensor_scalar(l1.ap, e.ap, 8.0, None, ALU.mult), reads=[e], writes=[l1])
                    K.op(K.dve, lambda: dve.tensor_copy(cst.ap[:, 3, :], l1.ap), reads=[l1], writes=[cst])
                    K.op(K.dve, lambda: dve.tensor_tensor(r1.ap, l1.ap, cst.ap[:, 3, :], ALU.subtract), reads=[l1, cst], writes=[r1])
                    K.op(K.dve, lambda: dve.tensor_copy(cst.ap[:, 4, :], r1.ap), reads=[r1], writes=[cst])
                    K.op(K.dve, lambda: dve.tensor_tensor(r2.ap, r1.ap, cst.ap[:, 4, :], ALU.subtract), reads=[r1, cst], writes=[r2])
                    K.op(K.dve, lambda: dve.tensor_copy(cst.ap[:, 5, :], r2.ap), reads=[r2], writes=[cst])
                    K.op(K.dve, lambda: dve.tensor_scalar(cst.ap[:, 0:3, :], cst.ap[:, 3:6, :], -1.0, None, ALU.mult), reads=[cst], writes=[cst])
                    K.dma(K.act, self.QT[:, 64:67, cols], cst.ap[:, 0:3, :], reads=[cst], writes=[self.QT_buf])
                    K.dma(K.act, self.KT[:, 67:70, cols], cst.ap[:, 3:6, :], reads=[cst], writes=[self.KT_buf])
                    K.dma(K.act, self.QT[:, 67:70, cols], cst1.ap, reads=[cst1], writes=[self.QT_buf])
                    K.dma(K.act, self.KT[:, 64:67, cols], cst1.ap, reads=[cst1], writes=[self.KT_buf])
                    self.vproj(L, j, "pv", xb, KC, bufs)
                else:
                    for g in range(5):
                        if g % 4 == 0:
                            slot = K.wnext(("pj", L, g // 4), 4096)
                        pm = self.psum[g % 4]
                        self.proj_group(slot, (g % 4) * 1024, None, xb, KC, 128, pm, None)
                        K.op(K.act, lambda: act.copy(lat[g].ap, pm.ap), reads=[pm], writes=[lat[g]])
                    slot = K.wnext(("pkr", L), 2 * KC * 96)
                    pm, psw = self.psum[0], self.psum[1]
                    self.proj_group(slot, 0, KC * 96, xb, KC, 96, pm, psw)
                    st = stage[0]
                    self.rope_evac(pm, psw, 96, rt[0], rt[1], st)
                    for h in range(H):
                        K.dma(K.act, self.KT[h, 64:96, cols], st.ap[64:96], reads=[st], writes=[self.KT_buf])
                    for (m0, nm, gc) in ((0, 3, 193), (3, 2, 196)):
                        ps_q = self.psum[4]
                        for mi in range(nm):
                            sq = self.tmp[2 + mi % 2]
                            K.op(K.act, lambda: act.activation(sq.ap, lat[m0 + mi].ap, AF.Square), reads=[lat[m0 + mi]], writes=[sq])
                            K.op(K.pe, lambda: K.pe.h.matmul(ps_q.ap, self.ones32.ap, sq.ap, start=(mi == 0), stop=(mi == nm - 1)),
                                 reads=[sq, self.ones32], writes=[ps_q])
                        var, std, rstd = self.stat[1], self.stat[2], self.stat[3]
                        K.op(K.dve, lambda: dve.tensor_scalar(var.ap, ps_q.ap, 1.0 / (nm * 128), RMS_EPS, ALU.mult, ALU.add),
                             reads=[ps_q], writes=[var])
                        K.op(K.act, lambda: act.activation(std.ap, var.ap, AF.Sqrt), reads=[var], writes=[std])
                        K.op(K.dve, lambda: dve.reciprocal(rstd.ap, std.ap), reads=[std], writes=[rstd])
                        for mi in range(nm):
                            K.op(K.dve, lambda: dve.scalar_tensor_tensor(latn[m0 + mi].ap, lat[m0 + mi].ap, self.prmcol(gc + mi), rstd.ap,
                                                                         ALU.mult, ALU.mult),
                                 reads=[lat[m0 + mi], rstd, self.prm_sb], writes=[latn[m0 + mi]])
                    for h in range(H):
                        if h % 4 == 0:
                            slot = K.wnext(("uq", L, h // 4), 4 * 576)
                        pm, psw = self.psum[(h % 2) * 2], self.psum[(h % 2) * 2 + 1]
                        off = (h % 4) * 576
                        self.proj_group(slot, off, off + 288, latn[0:3], 3, 96, pm, psw)
                        st = stage[h % 2]
                        self.rope_evac(pm, psw, 96, rt[0], rt[1], st, par=h)
                        K.dma(K.act, self.QT[h, 0:96, cols], st.ap[0:96], reads=[st], writes=[self.QT_buf])
                    slot = K.wnext(("uk", L), 8 * 256)
                    for i in range(8):
                        pm = self.psum[i % 4]
                        self.proj_group(slot, i * 256, None, latn[3:5], 2, 128, pm, None)
                        st = stage[i % 2]
                        K.op(K.act, lambda: act.copy(st.ap, pm.ap), reads=[pm], writes=[st])
                        self.store_heads(self.KT, self.KT_buf, st, i, j)
                    self.vproj(L, j, "uv", latn[3:5], 2, bufs)

            def ld(j):
                set_par(j)
                self.load_x(src, src_buf, j, True, q=K.act, do_prep=False)

            def prep(j):
                set_par(j)
                self.load_x(src, src_buf, j, True, do_dma=False)

            lnt = _arr(sb("lnt", [128, 2, T], F32), "lnt", 2)
            pending = []

            def ffn_front(j):
                set_par(j)
                self.ffn(L, 0, side=pending)
                while pending:
                    pending.pop(0)()

            def ffn_ln_back(j):
                set_par(j)

                def st(m, xm):
                    K.dma(K.act, self.xs[0][m * 128:(m + 1) * 128, j * T:(j + 1) * T], xm.ap, reads=[xm], writes=[self.xs_buf[0]])
                jobs = self.layernorm(gcol=L * 24 + 0, bcol=96 + L * 24 + 0, scale_after=False, defer=True, tbufs=lnt, after=st)
                pending.extend(jobs)

            def flush():
                while pending:
                    pending.pop(0)()

            NT = self.NT
            ld(0)
            prep(0)
            ffn_front(0)
            ffn_ln_back(0)
            flush()
            if NT > 1:
                ld(1)
                prep(1)
            for j in range(NT):
                K.maybe_epoch()
                if j + 1 < NT:
                    ffn_front(j + 1)
                if j + 2 < NT:
                    ld(j + 2)
                flush()
                set_par(j)
                proj(j)
                if j + 2 < NT:
                    prep(j + 2)
                if j + 1 < NT:
                    ffn_ln_back(j + 1)
                    if j + 2 >= NT:
                        flush()
            set_par(0)
            K.barrier(False)

    def attention(self, j, dk, scale, masked, ab):
        K = self.K
        act, dve, pool, pe = K.act.h, K.dve.h, K.pool.h, K.pe.h
        nk = (j + 1) * T
        nb = 4 * (j + 1)
        cols = slice(j * T, (j + 1) * T)
        ktb, vev, vod, qtb, pT, rec, bcs = ab["ktb"], ab["vev"], ab["vod"], ab["qtb"], ab["pT"], ab["rec"], ab["bcs"]
        pend = None
        for h in range(H):
            odd = h % 2
            kt = ktb[h % 2]
            qt = qtb[h % 2]
            vb = vod if odd else vev
            K.dma(K.sp, kt.ap[0:dk, 0:nk], self.KT[h, 0:dk, 0:nk], reads=[self.KT_buf], writes=[kt])
            K.dma(K.sp, qt.ap[0:dk, :], self.QT[h, 0:dk, cols], reads=[self.QT_buf], writes=[qt])
            vdst = vb.ap[:, 0:nb, 64:128] if odd else vb.ap[:, 0:nb, 0:64]
            K.dma(K.sp, vdst, self.VS[h, :, 0:nb, :], reads=[self.VS_buf], writes=[vb])
            ps_o = self.psum[4 + odd]
            SB = 4

            def smm(kb, kt=kt, qt=qt):
                self.mm(self.psum[kb % SB], [(kt.ap[:, kb * 128:(kb + 1) * 128], qt.ap[:, :])], reads=[kt, qt])
            for kb in range(min(SB, nb)):
                smm(kb)
            if pend is not None:
                pend()
            for kb in range(nb):
                ps_s = self.psum[kb % SB]
                p = pT[kb % 4]
                if masked is None and kb >= 4 * j:
                    a = kb - 4 * j
                    tt = self.tmp[kb % 2]
                    K.op(K.dve, lambda: dve.tensor_tensor(tt.ap, ps_s.ap, ab["cnT"].ap[:, a * 512:(a + 1) * 512], ALU.add),
                         reads=[ps_s, ab["cnT"]], writes=[tt])
                    K.op(K.act, lambda: act.activation(p.ap, tt.ap, AF.Exp, scale=scale), reads=[tt], writes=[p])
                else:
                    K.op(K.act, lambda: act.activation(p.ap, ps_s.ap, AF.Exp, scale=scale), reads=[ps_s], writes=[p])
                if masked is not None:
                    mk = masked.ap[:, kb, :]
                    K.op(K.dve, lambda: dve.tensor_tensor(p.ap, p.ap, mk, ALU.mult), reads=[p, masked], writes=[p])
                if odd:
                    lhsT, M = vb.ap[:, kb, 0:128], 128
                else:
                    lhsT, M = vb.ap[:, kb, 0:65], 65
                K.op(K.pe, lambda: pe.matmul(ps_o.ap[0:M, :], lhsT, p.ap, start=(kb == 0), stop=(kb == nb - 1)),
                     reads=[vb, p], writes=[ps_o])
                if kb + SB < nb:
                    smm(kb + SB)

            def epilogue(h=h, odd=odd, ps_o=ps_o):
                dr = 0 if odd else 64
                ps_b = self.psum[6]
                K.op(K.dve, lambda: dve.reciprocal(rec.ap[dr:dr + 1, :], ps_o.ap[dr:dr + 1, :]), reads=[ps_o], writes=[rec])
                K.op(K.pe, lambda: pe.matmul(ps_b.ap, self.ones32.ap[dr:dr + 1, :], rec.ap[dr:dr + 1, :], start=True, stop=True),
                     reads=[rec, self.ones32], writes=[ps_b])
                bc = bcs[h % 2]
                ob = 64 if odd else 0
                K.op(K.act, lambda: act.copy(bc.ap[ob:ob + 64], ps_b.ap[ob:ob + 64]), reads=[ps_b], writes=[bc])
                K.op(K.dve, lambda: dve.tensor_tensor(self.OT[h // 2].ap[ob:ob + 64], ps_o.ap[ob:ob + 64], bc.ap[ob:ob + 64], ALU.mult),
                     reads=[ps_o, bc], writes=[self.OT[h // 2]])
            pend = epilogue
        pend()

    def dsa_topk(self, j, ab):
        K = self.K
        act, dve, pool, pe = K.act.h, K.dve.h, K.pool.h, K.pe.h
        maskT, mk, qib, kib, wib, bis, bisg = (ab[k] for k in ("maskT", "mk", "qib", "kib", "wib", "bis", "bisg"))
        scs = [ab["sc"], ab["sc1"]]
        nkt = (j + 1) * T
        nbt = 4 * (j + 1)
        K.dma(K.sp, kib.ap[:, 0:nkt], self.KI[:, 0:nkt], reads=[self.KI_buf], writes=[kib])
        NIT = 20
        blocks = []
        for qs in range(4):
            i = 4 * j + qs
            qc = slice(qs * 128, (qs + 1) * 128)
            if i + 1 < nbt:
                K.op(K.pool, lambda: pool.memset(maskT.ap[:, i + 1:nbt, qc], 0.0), writes=[maskT])
            if i < 2:
                if i > 0:
                    K.op(K.pool, lambda: pool.memset(maskT.ap[:, 0:i, qc], 1.0), writes=[maskT])
                K.op(K.pool, lambda: pool.tensor_copy(maskT.ap[:, i, qc], self.cmb.ap[:, 0:128]), reads=[self.cmb], writes=[maskT])
            else:
                blocks.append(qs)

        def scores(qs):
            i = 4 * j + qs
            nk = 128 * (i + 1)
            sc = scs[qs % 2]
            K.dma(K.sp, qib.ap, self.QI[:, :, i * 128:(i + 1) * 128].rearrange("h d t -> d h t"), reads=[self.QI_buf], writes=[qib])
            K.dma(K.sp, wib.ap, self.WI[i * 128:(i + 1) * 128, :], reads=[self.WI_buf], writes=[wib])
            for g in range((nk + 511) // 512):
                n = min(512, nk - g * 512)
                scg = sc.ap[:, g * 512:g * 512 + n]
                for h in range(8):
                    ps = self.psum[2 + h % 2]
                    self.mm(ps, [(qib.ap[:, h, :], kib.ap[:, g * 512:g * 512 + n])], reads=[qib, kib], N=n)
                    r = self.tmp[h % 2]
                    K.op(K.act, lambda: act.activation(r.ap[:, 0:n], ps.ap[:, 0:n], AF.Relu), reads=[ps], writes=[r])
                    if h == 0:
                        K.op(K.pool, lambda: pool.tensor_scalar(scg, r.ap[:, 0:n], wib.ap[:, 0:1], 0.0, ALU.mult, ALU.add),
                             reads=[r, wib], writes=[sc])
                    else:
                        tp = self.tmp[2 + h % 2]
                        K.op(K.pool, lambda: pool.tensor_scalar(tp.ap[:, 0:n], r.ap[:, 0:n], wib.ap[:, h:h + 1], 0.0, ALU.mult, ALU.add),
                             reads=[r, wib], writes=[tp])
                        K.op(K.pool, lambda: pool.tensor_tensor(scg, scg, tp.ap[:, 0:n], ALU.add), reads=[sc, tp], writes=[sc])

        def select(qs):
            i = 4 * j + qs
            nk = 128 * (i + 1)
            qc = slice(qs * 128, (qs + 1) * 128)
            sc = scs[qs % 2]
            scn = sc.ap[:, 0:nk]
            lo, w0, mid, cnt, mx = (bis.ap[:, c:c + 1] for c in range(5))
            K.op(K.dve, lambda: dve.tensor_reduce(mx, scn, AX.X, ALU.max), reads=[sc], writes=[bis])
            K.op(K.dve, lambda: dve.tensor_reduce(lo, scn, AX.X, ALU.min), reads=[sc], writes=[bis])
            K.op(K.dve, lambda: dve.tensor_tensor(sc.ap[:, i * 128:(i + 1) * 128], sc.ap[:, i * 128:(i + 1) * 128], self.cneg.ap, ALU.add),
                 reads=[sc, self.cneg], writes=[sc])
            K.op(K.dve, lambda: dve.tensor_tensor(w0, mx, lo, ALU.subtract), reads=[bis], writes=[bis])
            K.op(K.dve, lambda: dve.tensor_scalar(w0, w0, 1.001, 1e-20, ALU.mult, ALU.add), reads=[bis], writes=[bis])
            for it in range(NIT):
                K.op(K.dve, lambda: dve.tensor_scalar(mid, w0, 2.0 ** (-(it + 1)), lo, ALU.mult, ALU.add), reads=[bis], writes=[bis])
                K.op(K.dve, lambda: dve.tensor_scalar(mk.ap[:, 0:nk], scn, mid, None, ALU.is_ge, ALU.add, accum_out=cnt),
                     reads=[sc, bis], writes=[mk, bis])
                K.op(K.dve, lambda: dve.tensor_scalar(bisg.ap[:, 0:1], cnt, float(TOPK), None, ALU.is_ge), reads=[bis], writes=[bisg])
                K.op(K.dve, lambda: dve.copy_predicated(lo, bisg.ap[:, 0:1], mid), reads=[bis, bisg], writes=[bis])
            K.op(K.dve, lambda: dve.tensor_scalar(mk.ap[:, 0:nk], scn, lo, None, ALU.is_ge), reads=[sc, bis], writes=[mk])
            pst = self.psum[7]
            pstb = pst.ap.bitcast(BF16)
            for kb0 in range(0, i + 1, 4):
                nbk = min(4, i + 1 - kb0)

                def tr():
                    ins = None
                    for t in range(nbk):
                        ins = pe.transpose(pstb[:, t * 128:(t + 1) * 128], mk.ap[:, (kb0 + t) * 128:(kb0 + t + 1) * 128],
                                           self.cmb.ap[:, 2048:2176])
                    return ins
                K.op(K.pe, tr, reads=[mk, self.cmb], writes=[pst])
                K.op(K.act, lambda: act.copy(maskT.ap[:, kb0:kb0 + nbk, qc], pstb[:, 0:nbk * 128].rearrange("p (a t) -> p a t", t=128)),
                     reads=[pst], writes=[maskT])

        if blocks:
            scores(blocks[0])
        for bi, qs in enumerate(blocks):
            if bi + 1 < len(blocks):
                scores(blocks[bi + 1])
            select(qs)

    def phase_b(self, L, dst, dst_buf, last):
        from contextlib import ExitStack
        K, nc = self.K, self.nc
        act, dve, pool = K.act.h, K.dve.h, K.pool.h
        kind = L % 3
        S = self.S
        dk, scale = ((64, 64 ** -0.5), (70, 64 ** -0.5), (96, 96 ** -0.5))[kind]
        with ExitStack() as es:
            sb = lambda name, shape, dtype: es.enter_context(nc.sbuf_tensor(name + "_b%d" % L, shape, dtype)).ap()
            ab = {}
            ab["ktb"] = _arr(sb("ktb", [128, 2, S], BF16), "ktb", 2)
            ab["vev"] = Buf(sb("vev", [128, self.NB, 80], BF16), "vev")
            ab["vod"] = Buf(sb("vod", [128, self.NB, 128], BF16), "vod")
            ab["qtb"] = _arr(sb("qtb", [128, 2, T], BF16), "qtb", 2)
            ab["pT"] = _arr(sb("pT", [128, 4, T], BF16), "pT", 4)
            ab["rec"] = self.stat[0]
            ab["bcs"] = [self.stat[1], self.stat[2]]
            self.OT = _arr(sb("OT", [128, KC, T], BF16), "OT", KC)
            for b_ in ab["ktb"] + ab["qtb"]:
                K.op(K.pool, lambda: pool.memset(b_.ap, 0.0), writes=[b_])
            K.op(K.pool, lambda: pool.memset(ab["vev"].ap[:, :, 64:65], 1.0), writes=[ab["vev"]])
            K.op(K.pool, lambda: pool.memset(ab["vod"].ap[:, :, 0:64], 0.0), writes=[ab["vod"]])
            K.op(K.pool, lambda: pool.memset(ab["vod"].ap[:, :, 0:1], 1.0), writes=[ab["vod"]])
            if kind != 0:
                ab["cnT"] = Buf(sb("cnT", [128, 4 * T], F32), "cnT")
                K.dma(K.sp, ab["cnT"].ap, self.cnT_in, writes=[ab["cnT"]])
            if kind == 0:
                ab["maskT"] = Buf(sb("maskT", [128, self.NB, T], BF16), "maskT")
                ab["sc"] = Buf(self.z_full, "sc")
                ab["mk"] = Buf(self.gT_full[:, 0:S], "mk")
                ab["sc1"] = Buf(self.x_full, "sc1")
                ab["sc1"].alias = self.x
                for b_ in self.x:
                    b_.alias = [ab["sc1"]]
                ab["sc"].alias = self.z
                ab["mk"].alias = self.gT[0:(2 * S + T * 2 - 1) // (T * 2)]
                for b in ab["sc"].alias:
                    b.alias = [ab["sc"]]
                for b in ab["mk"].alias:
                    b.alias = [ab["mk"]]
                ab["qib"] = Buf(sb("qib", [64, 8, 128], BF16), "qib")
                ab["kib"] = Buf(sb("kib", [64, S], BF16), "kib")
                ab["wib"] = Buf(sb("wib", [128, 8], F32), "wib")
                ab["bis"] = Buf(sb("bis", [128, 8], F32), "bis")
                ab["bisg"] = Buf(sb("bisg", [128, 2], mybir.dt.uint32), "bisg")
            for j in range(self.NT):
                K.maybe_epoch()
                masked = None
                if kind == 0:
                    self.dsa_topk(j, ab)
                    masked = ab["maskT"]
                self.load_x(self.xs[0], self.xs_buf[0], j, False)
                self.attention(j, dk, scale, masked, ab)
                for m in range(KC):
                    if m % 2 == 0:
                        slot = K.wnext(("wo", L, m // 2), 2048)
                    ps = self.psum[m % 2]
                    off = (m % 2) * 1024
                    self.mm(ps, [(slot.ap[:, off + c * 128: off + (c + 1) * 128], self.OT[c].ap) for c in range(KC)],
                            reads=[slot] + self.OT)
                    K.op(K.dve, lambda: dve.tensor_tensor(self.z[m].ap, ps.ap, self.x[m].ap, ALU.add),
                         reads=[ps, self.x[m]], writes=[self.z[m]])
                self.layernorm(gcol=L * 24 + 8, bcol=96 + L * 24 + 8, scale_after=True)
                self.ffn(L, 1)
                self.layernorm(gcol=L * 24 + 16, bcol=96 + L * 24 + 16, scale_after=False)
                self.store_x(dst, dst_buf, j)
            K.barrier(False)
            if kind == 0:
                for b in self.z + self.gT + self.x:
                    b.alias = []

    def emit(self):
        K, nc = self.K, self.nc
        pool, dve = K.pool.h, K.dve.h
        CH = 128 * 8192
        for i in range(self.wtotal // CH):
            K.dma(K.pool, K.wbf[i * CH:(i + 1) * CH].rearrange("(p n) -> p n", p=128),
                  self.wsrc[i * CH:(i + 1) * CH].rearrange("(p n) -> p n", p=128), reads=[], writes=[K.wbf_bufs[i]])
        K.dma(K.sp, self.prm_sb.ap, self.prm, writes=[self.prm_sb])
        K.dma(K.sp, self.cmb.ap, self.cmb_in, writes=[self.cmb])
        K.dma(K.sp, self.cneg.ap, self.cneg_in, writes=[self.cneg])
        K.op(K.pool, lambda: pool.memset(self.ones32.ap, 1.0), writes=[self.ones32])
        K.op(K.dve, lambda: dve.tensor_scalar(self.prm2_sb.ap, self.prm_sb.ap, ALPHA, None, ALU.mult), reads=[self.prm_sb], writes=[self.prm2_sb])
        for b in (self.ones32, self.cmb, self.cneg, self.prm_sb, self.prm2_sb):
            b.const = True
        for b in K.wbf_bufs:
            b.const = True
        if self.start_barrier:
            K.barrier(False)

        src, src_buf = self.xin, self.xin_buf
        for li, L in enumerate(self.layers):
            last = (li == len(self.layers) - 1)
            if self.stop_after == "a" and last:
                self.phase_a(L, src, src_buf)
                for j in range(self.NT):
                    self.load_x(self.xs[0], self.xs_buf[0], j, False)
                    self.store_x(self.xout, self.xout_buf, j)
                break
            self.phase_a(L, src, src_buf)
            if last:
                self.phase_b(L, self.xout, self.xout_buf, True)
            else:
                self.phase_b(L, self.xs[1], self.xs_buf[1], False)
                src, src_buf = self.xs[1], self.xs_buf[1]
        K.barrier(False)


def _wt(W, cols, nk):
    cols = np.asarray(cols)
    return W[:nk * 128][:, cols].reshape(nk, 128, len(cols)).transpose(1, 0, 2)


def _swap_idx(base, width, rot):
    idx = np.arange(base, base + width)
    h = rot // 2
    idx[:h] = np.arange(base + h, base + rot)
    idx[h:rot] = np.arange(base, base + h)
    return idx


def host_block(key, inp):
    kind = key[0]
    if kind == "w13":
        _, L, which, cp = key
        w = inp["ffn1_w13" if which == 0 else "ffn2_w13"][L]
        out = np.empty((128, 2, 2, KC, 128), np.float32)
        for c2 in range(2):
            c = cp * 2 + c2
            for gu in range(2):
                out[:, c2, gu] = _wt(w, np.arange(gu * DFF + c * 128, gu * DFF + (c + 1) * 128), KC)
        return out.reshape(128, -1)
    if kind == "w2":
        _, L, which, m = key
        w = inp["ffn1_w2" if which == 0 else "ffn2_w2"][L]
        return _wt(w, np.arange(m * 128, (m + 1) * 128), FC).reshape(128, -1)
    if kind == "wo":
        _, L, mp = key
        w = inp["w_out"][L]
        out = np.empty((128, 2, KC, 128), np.float32)
        for m2 in range(2):
            m = mp * 2 + m2
            out[:, m2] = _wt(w, np.arange(m * 128, (m + 1) * 128), KC)
        return out.reshape(128, -1)
    L = key[1]
    mk = L % 3
    if mk == 0:
        w = inp["dsa_w_in"][L // 3]
    elif mk == 1:
        w = inp["fox_w_in"][L // 3]
    else:
        w = inp["mla_w_dqkv"][L // 3]
    if kind == "pj" and mk == 0:
        gs = key[2]
        out = np.zeros((128, 2, 2, KC, 128), np.float32)
        for g2 in range(2):
            g = gs * 2 + g2
            if g > 20:
                continue
            if g < 8:
                bases = [g * 128, g * 128 + 64]
            elif g < 16:
                bases = [1024 + (g - 8) * 128, 1024 + (g - 8) * 128 + 64]
            elif g < 20:
                bases = [3072 + (g - 16) * 128, 3072 + (g - 16) * 128 + 64]
            else:
                bases = [3584, 3584]
            main = np.concatenate([np.arange(b0, b0 + 64) for b0 in bases])
            swap = np.concatenate([_swap_idx(b0, 64, 16) for b0 in bases])
            out[:, g2, 0] = _wt(w, main, KC)
            out[:, g2, 1] = _wt(w, swap, KC)
        return out.reshape(128, -1)
    if kind == "pj" and mk == 1:
        gs = key[2]
        out = np.zeros((128, 4, KC, 128), np.float32)
        for g4 in range(4):
            g = gs * 4 + g4
            base = g * 128 if g < 8 else 1024 + (g - 8) * 128
            out[:, g4] = _wt(w, np.arange(base, base + 128), KC)
        return out.reshape(128, -1)
    if kind == "pj" and mk == 2:
        gs = key[2]
        out = np.zeros((128, 4, KC, 128), np.float32)
        for g4 in range(4):
            g = gs * 4 + g4
            if g < 5:
                out[:, g4] = _wt(w, np.arange(g * 128, (g + 1) * 128), KC)
        return out.reshape(128, -1)
    if kind == "pf":
        return _wt(w, np.arange(3072, 3088), KC).reshape(128, -1)
    if kind == "pkr":
        out = np.zeros((128, 2, KC, 96), np.float32)
        out[:, 0, :, 64:96] = _wt(w, np.arange(640, 672), KC)
        out[:, 1, :, 64:96] = _wt(w, _swap_idx(640, 32, 32), KC)
        return out.reshape(128, -1)
    if kind == "pv":
        half = key[2]
        return _wt(w, np.arange(2048 + half * 512, 2048 + (half + 1) * 512), KC).reshape(128, -1)
    if kind == "pw":
        return _wt(w, np.arange(3648, 3656), KC).reshape(128, -1)
    if kind == "uq":
        hq = key[2]
        wq = inp["mla_w_uq"][L // 3]
        out = np.zeros((128, 4, 2, 3, 96), np.float32)
        for h4 in range(4):
            h = hq * 4 + h4
            main = np.arange(h * 96, (h + 1) * 96)
            swap = np.concatenate([np.arange(h * 96, h * 96 + 64), _swap_idx(h * 96 + 64, 32, 32)])
            out[:, h4, 0] = _wt(wq, main, 3)
            out[:, h4, 1] = _wt(wq, swap, 3)
        return out.reshape(128, -1)
    if kind == "uk":
        wk = inp["mla_w_ukv"][L // 3]
        out = np.zeros((128, 8, 2, 128), np.float32)
        for i in range(8):
            cols = np.concatenate([np.arange(2 * i * 128, 2 * i * 128 + 64), np.arange((2 * i + 1) * 128, (2 * i + 1) * 128 + 64)])
            out[:, i] = _wt(wk, cols, 2)
        return out.reshape(128, -1)
    if kind == "uv":
        wk = inp["mla_w_ukv"][L // 3]
        cols = np.concatenate([np.arange(h * 128 + 64, h * 128 + 128) for h in range(H)])
        return _wt(wk, cols, 2).reshape(128, -1)
    raise KeyError(key)


def host_consts(S):
    import ml_dtypes
    f32 = np.float32
    tabs = np.zeros((4, 128, S), f32)
    pos = np.arange(S, dtype=f32)
    inv = (f32(ROPE_THETA) ** (-np.arange(0, 16, 2, dtype=f32) / f32(16))).astype(f32)
    ang = (pos[None, :] * inv[:, None]).astype(f32)
    c, s_ = np.cos(ang).astype(f32), np.sin(ang).astype(f32)
    tabs[0] = 1.0
    for hb in (0, 64):
        tabs[0, hb:hb + 8] = c
        tabs[0, hb + 8:hb + 16] = c
        tabs[1, hb:hb + 8] = -s_
        tabs[1, hb + 8:hb + 16] = s_
    invm = (f32(ROPE_THETA) ** (-np.arange(0, 32, 2, dtype=f32) / f32(32))).astype(f32)
    angm = (pos[None, :] * invm[:, None]).astype(f32)
    cm_, sm_ = np.cos(angm).astype(f32), np.sin(angm).astype(f32)
    tabs[2] = 1.0
    tabs[2, 64:80] = cm_
    tabs[2, 80:96] = cm_
    tabs[3, 64:80] = -sm_
    tabs[3, 80:96] = sm_
    cmb = np.zeros((128, 4 * 512 + 128), f32)
    p = np.arange(128)[:, None]
    q = np.arange(512)[None, :]
    for a in range(4):
        cmb[:, a * 512:(a + 1) * 512] = ((a * 128 + p) <= q)
    cmb[:, 2048:2176] = np.eye(128, dtype=f32)
    cneg = np.where(np.arange(128)[None, :] <= p, 0.0, NEG).astype(f32)
    cnT = np.where(cmb[:, 0:2048] > 0, 0.0, NEG).astype(f32)
    return tabs, cmb.astype(ml_dtypes.bfloat16), cneg, cnT


_CACHE = {}


def get_prog(S, layers, stop_after=None):
    key = (S, tuple(layers), stop_after)
    if key not in _CACHE:
        pl = Prog(S, layers, True, stop_after=stop_after).plan
        pr = Prog(S, layers, False, plan=pl, stop_after=stop_after)
        _CACHE[key] = (pl, pr)
    return _CACHE[key]


def run(inputs, S=4096, layers=(0, 1, 2, 3), n_cores=8, trace=False, stop_after=None):
    plan, prog = get_prog(S, layers, stop_after)
    inp = {k: np.asarray(v) for k, v in inputs.items()}
    wsrc = np.zeros((plan["wtotal"],), np.float32)
    for key, (off, n) in plan["wblocks"].items():
        wsrc[off:off + 128 * n] = host_block(key, inp).reshape(-1)
    prm = np.zeros((128, 256), np.float32)
    prm[:, 0:96] = inp["ln_g"].reshape(DEPTH * 3 * KC, 128).T
    prm[:, 96:192] = inp["ln_b"].reshape(DEPTH * 3 * KC, 128).T
    prm[0:16, 192] = inp["fox_b_f"][0]
    prm[:, 193:196] = inp["mla_q_norm_g"][0].reshape(3, 128).T
    prm[:, 196:198] = inp["mla_kv_norm_g"][0].reshape(2, 128).T
    tabs, cmb, cneg, cnT = host_consts(S)
    x = inp["x"]
    in_maps = []
    for c in range(n_cores):
        in_maps.append({"xT": np.ascontiguousarray(x[c, :S].T), "wsrc": wsrc, "prm": prm, "tabs": tabs, "cmb_in": cmb, "cneg_in": cneg, "cnT_in": cnT})
    res = run_bass_kernel_spmd(prog.nc, in_maps, core_ids=list(range(n_cores)), trace=trace)
    out = np.stack([res.results[c]["outT"].T for c in range(n_cores)], axis=0)
    return out, res


def kernel(**inputs):
    out, _ = run(inputs)
    return np.ascontiguousarray(out.astype(np.float32))
```
